# Optimizing a Trainium2 kernel written in Bass

```python
import jax, jax.numpy as jnp
from jax import lax
import numpy as np

D_MODEL = 1024
BATCH = 8
SEQ = 4096
DEPTH = 1
DEC_BATCH = 2
DEC_SEQ = 8192
PAST_LEN = 128

GRID_W = 64
MIX_WIDTH = D_MODEL
ATTN_WIDTH = MIX_WIDTH // 2
FOURIER_WIDTH = MIX_WIDTH - ATTN_WIDTH
HEAD_DIM = 64
N_Q_HEADS = ATTN_WIDTH // HEAD_DIM
N_KV_HEADS = 2
GQA_GROUP = N_Q_HEADS // N_KV_HEADS
KV_WIDTH = N_KV_HEADS * HEAD_DIM
N_FOURIER_GROUPS = 4
FOURIER_GROUP_DIM = FOURIER_WIDTH // N_FOURIER_GROUPS
ROPE_THETA = 10000.0
Q_BLOCK = 128
EPS = 1e-6
SPLITS = [ATTN_WIDTH,
          ATTN_WIDTH + KV_WIDTH,
          ATTN_WIDTH + 2 * KV_WIDTH,
          2 * ATTN_WIDTH + 2 * KV_WIDTH,
          2 * ATTN_WIDTH + 2 * KV_WIDTH + FOURIER_WIDTH]
IN_WIDTH = 2 * ATTN_WIDTH + 2 * KV_WIDTH + 2 * FOURIER_WIDTH

kernel_name = "hymba_gqa_axialrope_fnet_encoder"


def _rmsnorm(x, w):
    xf = x.astype(jnp.float32)
    y = xf * lax.rsqrt(jnp.mean(xf * xf, axis=-1, keepdims=True) + EPS)
    return (y * w.astype(jnp.float32)).astype(x.dtype)


def _axial_rope_angles(seq_len):
    rows = seq_len // GRID_W
    row_idx, col_idx = jnp.meshgrid(jnp.arange(rows), jnp.arange(GRID_W), indexing="ij")
    row_idx = row_idx.reshape(-1).astype(jnp.float32)
    col_idx = col_idx.reshape(-1).astype(jnp.float32)
    axis_dim = HEAD_DIM // 2
    inv_freq = ROPE_THETA ** (-jnp.arange(0, axis_dim, 2, dtype=jnp.float32) / axis_dim)
    ang = jnp.concatenate([row_idx[:, None] * inv_freq, col_idx[:, None] * inv_freq], axis=-1)
    return jnp.cos(ang), jnp.sin(ang)


def _apply_rope(x, cos, sin):
    xf = x.astype(jnp.float32).reshape(*x.shape[:-1], HEAD_DIM // 2, 2)
    x1, x2 = xf[..., 0], xf[..., 1]
    c = cos[None, :, None, :]
    s = sin[None, :, None, :]
    out = jnp.stack([x1 * c - x2 * s, x1 * s + x2 * c], axis=-1).reshape(x.shape)
    return out.astype(x.dtype)


def _block_attention(q, k, v):
    B, S = q.shape[0], q.shape[1]
    nblk = S // Q_BLOCK
    qb = q.reshape(B, nblk, Q_BLOCK, N_KV_HEADS, GQA_GROUP, HEAD_DIM).transpose(1, 0, 2, 3, 4, 5)
    scale = HEAD_DIM ** -0.5

    def one_block(q_blk):
        s = jnp.einsum("bqkgd,bskd->bkgqs", q_blk, k, preferred_element_type=jnp.float32) * scale
        p = jax.nn.softmax(s, axis=-1).astype(v.dtype)
        return jnp.einsum("bkgqs,bskd->bqkgd", p, v)

    o = lax.map(one_block, qb)
    return o.transpose(1, 0, 2, 3, 4, 5).reshape(B, S, ATTN_WIDTH)


def _fourier_mix(u, w_f, b_f):
    B, S = u.shape[0], u.shape[1]
    ug = u.astype(jnp.float32).reshape(B, S, N_FOURIER_GROUPS, FOURIER_GROUP_DIM)
    mixed = jnp.real(jnp.fft.fft2(ug, axes=(1, 3), norm="ortho")).astype(u.dtype)
    out = jnp.einsum("bsgc,gcd->bsgd", mixed, w_f) + b_f
    return out.reshape(B, S, FOURIER_WIDTH)


def _mixer_layer(x, ln_w, w_in, q_norm, k_norm, w_f, b_f, w_out):
    B, S = x.shape[0], x.shape[1]
    h = _rmsnorm(x, ln_w)
    proj = jnp.einsum("bsd,de->bse", h, w_in)
    q, k, v, g_a, u_f, g_f = jnp.split(proj, SPLITS, axis=-1)
    cos, sin = _axial_rope_angles(S)
    q = _apply_rope(_rmsnorm(q.reshape(B, S, N_Q_HEADS, HEAD_DIM), q_norm), cos, sin)
    k = _apply_rope(_rmsnorm(k.reshape(B, S, N_KV_HEADS, HEAD_DIM), k_norm), cos, sin)
    v = v.reshape(B, S, N_KV_HEADS, HEAD_DIM)
    y_attn = _block_attention(q, k, v) * jax.nn.silu(g_a)
    y_four = _fourier_mix(u_f, w_f, b_f) * jax.nn.silu(g_f)
    y = jnp.concatenate([y_attn, y_four], axis=-1)
    return x + jnp.einsum("bse,ed->bsd", y, w_out)


def _trunk(x, ln_w, w_in, q_norm, k_norm, w_fourier, b_fourier, w_out, final_norm):
    for l in range(DEPTH):
        x = _mixer_layer(x, ln_w[l], w_in[l], q_norm[l], k_norm[l], w_fourier[l], b_fourier[l], w_out[l])
    return _rmsnorm(x, final_norm)


def setup_inputs(seed: int = 0) -> dict:
    key = jax.random.key(seed)
    ks = jax.random.split(key, 10)
    f32 = jnp.float32
    x_prompt = jax.random.normal(ks[0], (BATCH, SEQ, D_MODEL), f32)
    x_sample = jax.random.normal(ks[1], (DEC_BATCH, DEC_SEQ, D_MODEL), f32)
    ln_w = 1.0 + 0.02 * jax.random.normal(ks[2], (DEPTH, D_MODEL), f32)
    w_in = jax.random.normal(ks[3], (DEPTH, D_MODEL, IN_WIDTH), f32) * D_MODEL ** -0.5
    q_norm = 1.0 + 0.02 * jax.random.normal(ks[4], (DEPTH, HEAD_DIM), f32)
    k_norm = 1.0 + 0.02 * jax.random.normal(ks[5], (DEPTH, HEAD_DIM), f32)
    w_fourier = jax.random.normal(ks[6], (DEPTH, N_FOURIER_GROUPS, FOURIER_GROUP_DIM, FOURIER_GROUP_DIM), f32) * FOURIER_GROUP_DIM ** -0.5
    b_fourier = 0.02 * jax.random.normal(ks[7], (DEPTH, N_FOURIER_GROUPS, FOURIER_GROUP_DIM), f32)
    w_out = jax.random.normal(ks[8], (DEPTH, MIX_WIDTH, D_MODEL), f32) * MIX_WIDTH ** -0.5
    final_norm = 1.0 + 0.02 * jax.random.normal(ks[9], (D_MODEL,), f32)
    return {"x_prompt": x_prompt, "x_sample": x_sample, "ln_w": ln_w, "w_in": w_in,
            "q_norm": q_norm, "k_norm": k_norm, "w_fourier": w_fourier, "b_fourier": b_fourier,
            "w_out": w_out, "final_norm": final_norm}


def reference(x_prompt, x_sample, ln_w, w_in, q_norm, k_norm, w_fourier, b_fourier, w_out, final_norm):
    y_prompt = _trunk(x_prompt, ln_w, w_in, q_norm, k_norm, w_fourier, b_fourier, w_out, final_norm)
    y_sample = _trunk(x_sample, ln_w, w_in, q_norm, k_norm, w_fourier, b_fourier, w_out, final_norm)
    return (y_prompt, y_sample)
```

```python
import contextlib
import numpy as np
import ml_dtypes
import concourse.bass as bass
import concourse.mybir as mybir
from concourse.bass_utils import run_bass_kernel_spmd

F32 = mybir.dt.float32
BF16 = mybir.dt.bfloat16
AF = mybir.ActivationFunctionType
ALU = mybir.AluOpType
AX = mybir.AxisListType
bf = ml_dtypes.bfloat16

D = 1024
INW = 2304
EPS = 1e-6
N_CORES = 8


class Buf:
    __slots__ = ("name", "ws", "rs", "excl")

    def __init__(self, name="", excl=False):
        self.name = name
        self.ws = []
        self.rs = []
        self.excl = excl


class Op:
    __slots__ = ("eng", "fn", "deps", "ords", "sig", "cnt", "dma", "dsem", "dcnt", "c", "idx", "seg", "strm",
                 "t0", "t1", "nbytes", "done", "prio")

    def __init__(self, eng, fn, dma):
        self.eng = eng
        self.fn = fn
        self.deps = []
        self.ords = []
        self.sig = False
        self.cnt = 0
        self.dma = dma
        self.dsem = None
        self.dcnt = 0
        self.done = False


ENGS = ("pe", "act", "dve", "pool", "sp")


def _cost(eng, dma, n, nbytes):
    if dma:
        return 0.06
    if eng == "pe":
        return n / 1950.0 + 0.03
    if eng == "act":
        return (n + 150) / 1200.0
    if eng == "dve":
        return (n + 150) / 960.0
    return n * 0.002 + 0.2


class Prog:
    def __init__(self, nc, n_dma_sems=8):
        self.nc = nc
        self.ops = {e: [] for e in ENGS}
        self.n_dma_sems = n_dma_sems
        self.dma_rr = {e: 0 for e in ENGS}
        self.dma_last = {}
        self.seg = 0
        self.strm = 0
        self.nops = 0
        self.grp = None

    def op(self, eng, fn, reads=(), writes=(), dma=False, n=128, nbytes=0):
        o = Op(eng, fn, dma)
        o.c = _cost(eng, dma, n, nbytes)
        o.nbytes = nbytes
        o.idx = self.nops
        self.nops += 1
        o.seg = self.seg
        o.strm = self.strm
        o.prio = 0.0
        if self.grp is not None:
            self.grp.append(o)
        deps = o.deps
        excl_reads = [b for b in reads if b.excl]
        if excl_reads:
            reads = [b for b in reads if not b.excl]
            writes = list(writes) + [b for b in excl_reads if b not in writes]
        for b in reads:
            for w in b.ws:
                if w is not o:
                    deps.append(w)
            b.rs.append(o)
        for b in writes:
            for w in b.ws:
                if w is o:
                    continue
                if eng == "pe" and w.eng == "pe" and not w.dma and not dma:
                    o.ords.append(w)
                else:
                    deps.append(w)
            for r in b.rs:
                if r is not o:
                    deps.append(r)
            b.ws = [o]
            b.rs = []
        if dma:
            slot = self.dma_rr[eng]
            self.dma_rr[eng] = (slot + 1) % self.n_dma_sems
            prev = self.dma_last.get((eng, slot))
            if prev is not None:
                deps.append(prev)
            self.dma_last[(eng, slot)] = o
            o.dsem = (eng, slot)
        self.ops[eng].append(o)
        return o

    def group_begin(self):
        self.grp = []

    def group_end(self, p0, p1):
        g, self.grp = self.grp, None
        n = max(len(g), 1)
        for i, o in enumerate(g):
            o.prio = p0 + (p1 - p0) * i / n

    def barrier(self):
        self.seg += 1

    def schedule(self, W=96):
        HOP = 0.5
        BG_SLACK = 2.0
        self.seg_span = []
        allops = [o for e in ENGS for o in self.ops[e]]
        nseg = self.seg + 1
        new = {e: [] for e in ENGS}
        for sg in range(nseg):
            qs = {}
            for e in ENGS:
                for o in self.ops[e]:
                    if o.seg == sg:
                        qs.setdefault((e, o.strm), []).append(o)
            heads = {k: 0 for k in qs}
            free = {e: 0.0 for e in ENGS}
            dma_free = 0.0
            remaining = sum(len(v) for v in qs.values())
            out = {e: [] for e in ENGS}
            while remaining:
                best = None
                for (e, sid), lst in qs.items():
                    h = heads[(e, sid)]
                    while h < len(lst) and lst[h].done:
                        h += 1
                    heads[(e, sid)] = h
                    cnt = 0
                    i = h
                    fe = free[e]
                    while i < len(lst) and cnt < W:
                        o = lst[i]
                        i += 1
                        if o.done:
                            continue
                        cnt += 1
                        rdy = 0.0
                        ok = True
                        for d in o.deps:
                            if d.seg != sg:
                                continue
                            if not d.done:
                                ok = False
                                break
                            t = d.t1 + (HOP if d.eng != e or d.dma else 0.02)
                            if t > rdy:
                                rdy = t
                        if ok:
                            for d in o.ords:
                                if d.seg == sg and not d.done:
                                    ok = False
                                    break
                        if not ok:
                            continue
                        if sid == 1 and e == "pe":
                            rdy += BG_SLACK
                        stt = rdy if rdy > fe else fe
                        key = (stt, o.prio, o.idx)
                        if best is None or key < best[0]:
                            best = (key, o)
                        if stt <= fe:
                            break
                assert best is not None, "scheduler stuck (dependency cycle?)"
                (stt, _, _), o = best
                o.t0 = stt
                if o.dma:
                    xs = max(stt, dma_free)
                    dma_free = xs + o.nbytes / 150000.0
                    o.t1 = dma_free + 2.0
                    free[o.eng] = stt + o.c
                else:
                    o.t1 = stt + o.c
                    free[o.eng] = o.t1
                o.done = True
                out[o.eng].append(o)
                remaining -= 1
            for e in ENGS:
                new[e].extend(out[e])
            self.seg_span.append(max([o.t1 for e in ENGS for o in out[e]] + [0.0]))
        self.ops = new

    def emit(self):
        nc = self.nc
        nseg = self.seg + 1
        lastc = [dict() for _ in range(nseg)]
        lastd = [dict() for _ in range(nseg)]
        for e in ENGS:
            for o in self.ops[e]:
                if o.dma:
                    lastd[o.seg][o.dsem] = o
                else:
                    lastc[o.seg][e] = o
        for e in ENGS:
            cur = -1
            for o in self.ops[e]:
                if o.seg != cur:
                    for sg in range(max(cur, 0), o.seg):
                        o.deps.extend(lastc[sg].values())
                        o.deps.extend(lastd[sg].values())
                    cur = o.seg
        for e in ENGS:
            for o in self.ops[e]:
                for d in o.deps:
                    d.sig = True
        with contextlib.ExitStack() as st:
            esem = {e: st.enter_context(nc.semaphore("c_" + e)) for e in ENGS}
            dsem = {}
            for e in ENGS:
                if any(o.dma for o in self.ops[e]):
                    for s in range(self.n_dma_sems):
                        dsem[(e, s)] = st.enter_context(nc.semaphore("d_%s%d" % (e, s)))
            dcount = {k: 0 for k in dsem}
            for e in ENGS:
                c = 0
                for o in self.ops[e]:
                    if o.dma:
                        dcount[o.dsem] += 16
                        o.dcnt = dcount[o.dsem]
                    elif o.sig:
                        c += 1
                        o.cnt = c
            block = st.enter_context(nc.Block())
            engobj = {"pe": block.tensor, "act": block.scalar, "dve": block.vector,
                      "pool": block.gpsimd, "sp": block.sync}

            def make(e):
                ops = self.ops[e]

                def body(eng):
                    seen = {}
                    for o in ops:
                        need = {}
                        for d in o.deps:
                            if d.dma:
                                k = ("d",) + d.dsem
                                v = d.dcnt
                                sem = dsem[d.dsem]
                            else:
                                k = ("c", d.eng)
                                v = d.cnt
                                sem = esem[d.eng]
                            if seen.get(k, 0) >= v:
                                continue
                            if k not in need or need[k][1] < v:
                                need[k] = (sem, v)
                        for k, (sem, v) in need.items():
                            eng.wait_ge(sem, v)
                            seen[k] = v
                        ins = o.fn(eng)
                        if o.dma:
                            ins.then_inc(dsem[o.dsem], 16)
                        elif o.sig:
                            ins.then_inc(esem[e], 1)
                    if e == "sp":
                        for k, sem in dsem.items():
                            if dcount[k] > 0:
                                eng.wait_ge(sem, dcount[k])
                return body

            for e in ENGS:
                if self.ops[e] or e == "sp":
                    engobj[e](make(e))


def rope_tab(tok):
    tok = np.asarray(tok)
    row = (tok // 64).astype(np.float32)
    col = (tok % 64).astype(np.float32)
    inv = (np.float32(10000.0) ** (-np.arange(0, 32, 2, dtype=np.float32) / np.float32(32))).astype(np.float32)
    ang = np.concatenate([row[..., None] * inv, col[..., None] * inv], axis=-1).astype(np.float32)
    return np.concatenate([np.cos(ang), np.sin(ang)], axis=-1).astype(np.float32)


def f1_mat():
    p = np.arange(128)[:, None]
    k = np.arange(128)[None, :]
    a = 2 * np.pi * p * k / 128.0
    return np.concatenate([np.cos(a), -np.sin(a)], axis=1) / np.sqrt(128.0)


def twiddle(J):
    S = 128 * J
    j = (np.arange(128) % J)[:, None]
    k1 = np.arange(128)[None, :]
    a = 2 * np.pi * j * k1 / S
    tr, ti = np.cos(a), -np.sin(a)
    return np.concatenate([tr, tr], 1).astype(np.float32), np.concatenate([ti, ti], 1).astype(np.float32)


def g_mats(J, k2list):
    r = 128 // J
    K2 = len(k2list)
    m = np.arange(128)
    cl = m // J
    j = m % J
    n1 = K2 * r
    Gr = np.zeros((128, n1))
    Gi = np.zeros((128, n1))
    for k2i, k2 in enumerate(k2list):
        for c in range(r):
            n = k2i * r + c
            sel = cl == c
            a = 2 * np.pi * j[sel] * k2 / J
            Gr[sel, n] = np.cos(a) / np.sqrt(J)
            Gi[sel, n] = -np.sin(a) / np.sqrt(J)
    return np.concatenate([Gr, Gi], 1), np.concatenate([-Gi, Gr], 1)


def cs_mats():
    c = np.arange(128)
    a = 2 * np.pi * np.outer(c, c) / 128.0
    return (np.cos(a) / np.sqrt(128.0)).astype(np.float32), (np.sin(a) / np.sqrt(128.0)).astype(np.float32)


JOBS = (
    dict(nm="s", S=8192, J=64, own=2048),
    dict(nm="p", S=4096, J=32, own=4096),
)


class _Stop(Exception):
    pass


def build(stop=None):
    nc = bass.Bass("TRN2", target_bir_lowering=False)
    dbg_out = {}

    def chk(name, dumps):
        if stop != name:
            return
        P.barrier()
        for i, (dn, ap, shape, dt) in enumerate(dumps):
            d = nc.dram_tensor("dbg_" + dn, list(shape), dt, kind="ExternalOutput").ap()
            dbg_out[dn] = d
            P.op("sp", lambda e, d=d, ap=ap: e.dma_start(out=d, in_=ap), dma=True)
        raise _Stop()

    def din(name, shape, dt=F32):
        return nc.dram_tensor(name, list(shape), dt, kind="ExternalInput").ap()

    xfull = {"s": din("xs", [8192, D]), "p": din("xp", [4096, D])}
    xown = {"s": din("xo", [2048, D]), "p": xfull["p"]}
    ln_w = din("ln_w", [D])
    w_in = din("w_in", [D, INW])
    q_norm = din("q_norm", [64])
    k_norm = din("k_norm", [64])
    w_f = din("w_f", [4, 128, 128])
    b_f = din("b_f", [4, 128])
    w_out = din("w_out", [D, D])
    fnorm = din("fnorm", [D])
    rk = {"s": din("rk_s", [64, 128, 64]), "p": din("rk_p", [32, 128, 64])}
    rq = {"s": din("rq_s", [16, 128, 64]), "p": din("rq_p", [32, 128, 64])}
    f1_d = din("f1", [128, 256], BF16)
    t1_d = {"s": din("t1_s", [128, 256]), "p": din("t1_p", [128, 256])}
    t2_d = {"s": din("t2_s", [128, 256]), "p": din("t2_p", [128, 256])}
    ga_d = {"s": din("ga_s", [128, 64], BF16), "p": din("ga_p", [128, 256], BF16)}
    gb_d = {"s": din("gb_s", [128, 64], BF16), "p": din("gb_p", [128, 256], BF16)}
    gn_d = {"s": din("gn_s", [128, 64], BF16), "p": din("gn_p", [128, 256], BF16)}
    cc_d = din("cc", [128, 128])
    sc_d = din("sc", [128, 128])
    yout = {"s": nc.dram_tensor("yo", [2048, D], F32, kind="ExternalOutput").ap(),
            "p": nc.dram_tensor("yp", [4096, D], F32, kind="ExternalOutput").ap()}

    ARENA_BYTES = 207 * 1024
    st = contextlib.ExitStack()
    arena = st.enter_context(nc.sbuf_tensor("arena", [128, ARENA_BYTES // 4], F32))
    psum = st.enter_context(nc.psum_tensor("psum", [128, 4096], F32))
    P = Prog(nc)

    class Alloc:
        def __init__(self, base=0):
            self.off = base

        def get(self, free_shape, dt):
            nel = int(np.prod(free_shape))
            sz = nel * (4 if dt == F32 else 2)
            sz = (sz + 63) // 64 * 64
            a = arena[:, self.off // 4:(self.off + sz) // 4]
            self.off += sz
            assert self.off <= ARENA_BYTES, ("SBUF arena overflow", self.off)
            if dt != F32:
                a = a.bitcast(dt)
            a = a[:, 0:nel]
            if len(free_shape) == 2:
                a = a.rearrange("p (a b) -> p a b", b=free_shape[1])
            elif len(free_shape) == 3:
                a = a.rearrange("p (a b c) -> p a b c", b=free_shape[1], c=free_shape[2])
            return a

    def bank(i, n=1):
        return psum[:, 512 * i:512 * (i + n)]

    PB = [Buf("pb%d" % i, excl=True) for i in range(8)]

    A0 = Alloc(0)
    ident = A0.get((128,), BF16)
    f1 = A0.get((256,), BF16)
    lnw_col = A0.get((8,), F32)
    gq = A0.get((64,), F32)
    gk = A0.get((64,), F32)
    bfc = A0.get((4,), F32)
    fn_bc = A0.get((1024,), F32)
    qc = A0.get((4, 2, 128), BF16)
    cc_t = A0.get((128,), F32)
    sc_t = A0.get((128,), F32)
    epsc = A0.get((1,), F32)
    wf_t = A0.get((4, 128), F32)
    neg1 = A0.get((1,), F32)
    B_ident, B_f1, B_small, B_fn, B_qc, B_cs = Buf(), Buf(), Buf(), Buf(), Buf(), Buf()
    PERSIST_END = A0.off

    P.op("pool", lambda e: e.memset(ident, 1.0), writes=[B_ident])
    P.op("pool", lambda e: e.affine_select(ident, ident, [[-1, 128]], ALU.is_equal, 0.0, base=0, channel_multiplier=1),
         reads=[B_ident], writes=[B_ident])
    P.op("pool", lambda e: e.memset(epsc, EPS), writes=[B_small])
    P.op("pool", lambda e: e.memset(neg1, -1.0), writes=[B_small])
    P.op("pool", lambda e: e.dma_start(out=f1, in_=f1_d), writes=[B_f1], dma=True)
    P.op("pool", lambda e: e.dma_start(out=lnw_col, in_=ln_w.rearrange("(k p) -> p k", p=128), allow_slow_non_contiguous=True),
         writes=[B_small], dma=True)
    P.op("pool", lambda e: e.dma_start(out=gq, in_=q_norm.partition_broadcast(128)), writes=[B_small], dma=True)
    P.op("pool", lambda e: e.dma_start(out=gk, in_=k_norm.partition_broadcast(128)), writes=[B_small], dma=True)
    P.op("pool", lambda e: e.dma_start(out=bfc, in_=b_f.rearrange("g d -> d g"), allow_slow_non_contiguous=True),
         writes=[B_small], dma=True)
    P.op("pool", lambda e: e.dma_start(out=fn_bc, in_=fnorm.partition_broadcast(128)), writes=[B_fn], dma=True)
    P.op("pool", lambda e: e.dma_start(out=cc_t, in_=cc_d), writes=[B_cs], dma=True)
    P.op("pool", lambda e: e.dma_start(out=sc_t, in_=sc_d), writes=[B_cs], dma=True)

    B_wf = Buf()
    P.op("pool", lambda e: e.dma_start(out=wf_t, in_=w_f.rearrange("g c d -> c g d")), writes=[B_wf], dma=True)
    for g in range(4):
        for ri, cst in enumerate((cc_t, sc_t)):
            bk = (g * 2 + ri) % 8
            P.op("pe", lambda e, cst=cst, g=g, bk=bk: e.matmul(bank(bk)[:, 0:128], cst, wf_t[:, g, :], start=True, stop=True),
                 reads=[B_cs, B_wf], writes=[PB[bk]])
            P.op("dve", lambda e, g=g, ri=ri, bk=bk: e.tensor_copy(qc[:, g, ri, :], bank(bk)[:, 0:128]),
                 reads=[PB[bk]], writes=[B_qc])
    stopped = False
    try:
        chk("setup", [("qc", qc.rearrange("p a b c -> p (a b c)"), [128, 1024], BF16), ("ident", ident, [128, 128], BF16),
                      ("lnw", lnw_col, [128, 8], F32), ("gq", gq, [128, 64], F32), ("bfc", bfc, [128, 4], F32), ("fn", fn_bc, [128, 1024], F32)])
    except _Stop:
        stopped = True

    def rms_rstd(x_ap, x_buf, junk, junk_buf, ss, lnv, rstd, sbuf, ncols, sq_eng="act"):
        if sq_eng == "act":
            P.op("act", lambda e: e.activation(junk, x_ap, AF.Square, accum_out=ss), reads=[x_buf], writes=[junk_buf, sbuf], n=ncols)
        else:
            P.op("dve", lambda e: e.tensor_tensor(junk, x_ap, x_ap, ALU.mult), reads=[x_buf], writes=[junk_buf], n=ncols)
            P.op("dve", lambda e: e.tensor_reduce(ss, junk, AX.X, ALU.add), reads=[junk_buf], writes=[sbuf], n=ncols)
        P.op("act", lambda e: e.activation(lnv, ss, AF.Ln, bias=epsc, scale=1.0 / ncols), reads=[sbuf, B_small], writes=[sbuf], n=1)
        P.op("act", lambda e: e.activation(rstd, lnv, AF.Exp, scale=-0.5), reads=[sbuf], writes=[sbuf], n=1)

    def normrope(src, src_bufs, T, H, gains, tab, tab_buf, outb, out_buf, wk, reng="pool", evac=False):
        N = T * H * 64
        xf, sq, ta, tb = wk["xf"][:, 0:N], wk["sq"][:, 0:N], wk["ta"][:, 0:N // 2], wk["tb"][:, 0:N // 2]
        ssq, lnq, rsq = wk["ssq"][:, 0:T * H], wk["lnq"][:, 0:T * H], wk["rsq"][:, 0:T * H]
        BW = wk["buf"]
        xf3 = xf.rearrange("p (t c) -> p t c", t=T)
        sq3 = sq.rearrange("p (t c) -> p t c", t=T)
        if evac:
            P.op("dve", lambda e, psrc=src: e.tensor_copy(xf3, psrc), reads=src_bufs, writes=[BW["xf"]], n=N)
            src, src_bufs = xf3, [BW["xf"]]
        if evac:
            P.op("dve", lambda e: e.tensor_tensor(sq3, src, src, ALU.mult), reads=src_bufs, writes=[BW["sq"]], n=N)
        else:
            P.op("act", lambda e: e.activation(sq3, src, AF.Square), reads=src_bufs, writes=[BW["sq"]], n=N)
        P.op("dve", lambda e: e.tensor_reduce(ssq, sq.rearrange("p (h d) -> p h d", d=64), AX.X, ALU.add),
             reads=[BW["sq"]], writes=[BW["st"]], n=N)
        P.op("act", lambda e: e.activation(lnq, ssq, AF.Ln, bias=epsc, scale=1.0 / 64), reads=[BW["st"], B_small], writes=[BW["st"]], n=8)
        P.op("act", lambda e: e.activation(rsq, lnq, AF.Exp, scale=-0.5), reads=[BW["st"]], writes=[BW["st"]], n=8)
        P.op("dve", lambda e: e.tensor_tensor(sq3.rearrange("p t (h d) -> p t h d", d=64), src.rearrange("p t (h d) -> p t h d", d=64),
                                              rsq.rearrange("p (t h) -> p t h", t=T).unsqueeze(3).to_broadcast([128, T, H, 64]), ALU.mult),
             reads=src_bufs + [BW["st"]], writes=[BW["sq"]], n=N)
        P.op("dve", lambda e: e.tensor_tensor(xf.rearrange("p (h d) -> p h d", d=64), sq.rearrange("p (h d) -> p h d", d=64),
                                              gains.unsqueeze(1).to_broadcast([128, T * H, 64]), ALU.mult),
             reads=[BW["sq"], B_small], writes=[BW["xf"]], n=N)
        x4 = xf.rearrange("p (t h i two) -> p t h i two", t=T, h=H, two=2)
        o4 = outb.rearrange("p (t h i two) -> p t h i two", t=T, h=H, two=2)
        x1, x2 = x4[:, :, :, :, 0], x4[:, :, :, :, 1]
        cosb = tab[:, :, 0:32].unsqueeze(2).to_broadcast([128, T, H, 32])
        sinb = tab[:, :, 32:64].unsqueeze(2).to_broadcast([128, T, H, 32])
        ta4 = ta.rearrange("p (t h i) -> p t h i", t=T, h=H)
        tb4 = tb.rearrange("p (t h i) -> p t h i", t=T, h=H)
        h2 = N // 2
        P.op(reng, lambda e: e.tensor_tensor(ta4, x1, cosb, ALU.mult), reads=[BW["xf"], tab_buf], writes=[BW["ta"]], n=h2)
        P.op(reng, lambda e: e.tensor_tensor(tb4, x2, sinb, ALU.mult), reads=[BW["xf"], tab_buf], writes=[BW["tb"]], n=h2)
        P.op(reng, lambda e: e.tensor_tensor(o4[:, :, :, :, 0], ta4, tb4, ALU.subtract), reads=[BW["ta"], BW["tb"]], writes=[out_buf], n=h2)
        P.op(reng, lambda e: e.tensor_tensor(ta4, x1, sinb, ALU.mult), reads=[BW["xf"], tab_buf], writes=[BW["ta"]], n=h2)
        P.op(reng, lambda e: e.tensor_tensor(tb4, x2, cosb, ALU.mult), reads=[BW["xf"], tab_buf], writes=[BW["tb"]], n=h2)
        P.op(reng, lambda e: e.tensor_tensor(o4[:, :, :, :, 1], ta4, tb4, ALU.add), reads=[BW["ta"], BW["tb"]], writes=[out_buf], n=h2)

    wq = [0]

    def load_convert_win(dst, dst_buf, kc, sections, stage, stage_buf):
        for (c0, n, d0, perm) in sections:
            wq[0] += 1
            P.op("sp" if wq[0] % 2 else "pool", lambda e, c0=c0, n=n: e.dma_start(out=stage[:, 0:n], in_=w_in[kc * 128:(kc + 1) * 128, c0:c0 + n]),
                 writes=[stage_buf], dma=True, nbytes=128 * n * 4)
            if perm:
                o = dst[:, kc, d0:d0 + n].rearrange("p (hp ab d) -> p hp ab d", hp=4, ab=2)
                i = stage[:, 0:n].rearrange("p (ab hp d) -> p hp ab d", hp=4, ab=2)
            else:
                o = dst[:, kc, d0:d0 + n]
                i = stage[:, 0:n]
            P.op("dve", lambda e, o=o, i=i: e.tensor_scalar(o, i, lnw_col[:, kc:kc + 1], None, ALU.mult),
                 reads=[stage_buf, B_small], writes=[dst_buf], n=n)

    def wk_set(A, N):
        return dict(xf=A.get((N,), F32), sq=A.get((N,), F32), ta=A.get((N // 2,), F32), tb=A.get((N // 2,), F32),
                    ssq=A.get((16,), F32), lnq=A.get((16,), F32), rsq=A.get((16,), F32),
                    buf=dict(xf=Buf(), sq=Buf(), ta=Buf(), tb=Buf(), st=Buf()))

    def run_job(job):
        nm, S, J, OWN = job["nm"], job["S"], job["J"], job["own"]
        r = 128 // J
        NT_OWN = OWN // 128
        K2 = NT_OWN
        N2 = 2 * K2 * r
        NCH = OWN // 512
        xf_d, xo_d, y_d = xfull[nm], xown[nm], yout[nm]
        x_str = xf_d.rearrange("(p j) c -> j p c", j=J)
        x_nat = xo_d.rearrange("(t p) c -> t p c", p=128)
        y_nat = y_d.rearrange("(t p) c -> t p c", p=128)
        P.strm = 0

        AJ = Alloc(PERSIST_END)
        kT = AJ.get((S,), BF16)
        vx = AJ.get((J, 192), BF16)
        foT = AJ.get((4, OWN), BF16)
        t1 = AJ.get((256,), F32)
        t2 = AJ.get((256,), F32)
        ga = AJ.get((N2,), BF16)
        gb = AJ.get((N2,), BF16)
        B_kT = [Buf() for _ in range(J)]
        B_vx = [Buf() for _ in range(J)]
        B_fo = [Buf() for _ in range(NCH)]
        B_tw, B_g = Buf(), Buf()
        JOB_END = AJ.off

        P.op("pool", lambda e: e.memset(vx, 1.0), writes=B_vx, n=J * 192)
        P.op("pool", lambda e: e.dma_start(out=t1, in_=t1_d[nm]), writes=[B_tw], dma=True, nbytes=131072)
        P.op("pool", lambda e: e.dma_start(out=t2, in_=t2_d[nm]), writes=[B_tw], dma=True, nbytes=131072)
        P.op("pool", lambda e: e.dma_start(out=ga, in_=ga_d[nm]), writes=[B_g], dma=True, nbytes=65536)
        P.op("pool", lambda e: e.dma_start(out=gb, in_=gb_d[nm]), writes=[B_g], dma=True, nbytes=65536)

        AF_ = Alloc(JOB_END)
        U = AF_.get((J, 512), BF16)
        xg = [AF_.get((K2, 2, 128), BF16) for _ in range(2)]
        F2_BASE = AF_.off
        w1 = AF_.get((8, 768), BF16)
        NXT = 3
        xt = [AF_.get((1024,), F32) for _ in range(NXT)]
        hb = [AF_.get((1024,), BF16) for _ in range(2)]
        hT = [AF_.get((8, 128), BF16) for _ in range(2)]
        junk = AF_.get((1024,), BF16)
        st_s = [AF_.get((4,), F32) for _ in range(2)]
        rt = [AF_.get((2, 64), F32) for _ in range(2)]
        wkk = [wk_set(AF_, 256) for _ in range(2)]
        kb = [AF_.get((256,), BF16) for _ in range(2)]
        _as = Alloc(JOB_END + J * 512 * 2)
        NSTG = 8 if nm == "s" else 8
        stage = [_as.get((512,), F32) for _ in range(NSTG)]
        AF2 = Alloc(F2_BASE)
        NS1 = 3
        p12 = [AF2.get((2, 2, 256), BF16) for _ in range(NS1)]
        y2 = [AF2.get((2, 2, 128), BF16) for _ in range(NS1)]
        xT3 = [AF2.get((2, 512), BF16) for _ in range(2)]
        gn = AF2.get((N2,), BF16)
        y1s = [AF2.get((2, 256), BF16) for _ in range(NS1)]
        t1b = AF2.get((256,), BF16)
        t2b = AF2.get((256,), BF16)
        B_y1s = [Buf() for _ in range(NS1)]
        B_p2 = [Buf() for _ in range(NS1)]
        B_p2b = [Buf() for _ in range(NS1)]
        B_twb = Buf()
        B_w1, B_U = Buf(), Buf()
        B_xg = [Buf(), Buf()]
        B_xt = [Buf() for _ in range(NXT)]
        B_hb, B_hT, B_st, B_rt, B_kb = ([Buf(), Buf()] for _ in range(5))
        B_junk = Buf()
        B_p12 = [Buf() for _ in range(NS1)]
        B_stage = [Buf() for _ in range(NSTG)]
        B_y2 = [Buf() for _ in range(NS1)]
        B_xT3 = [Buf(), Buf()]

        for kc in range(8):
            load_convert_win(w1, B_w1, kc, [(512, 256, 0, False)], stage[(2 * kc) % NSTG], B_stage[(2 * kc) % NSTG])
            load_convert_win(w1, B_w1, kc, [(1280, 512, 256, False)], stage[(2 * kc + 1) % NSTG], B_stage[(2 * kc + 1) % NSTG])

        chk(nm + "_w1", [("w1", w1.rearrange("p a b -> p (a b)"), [128, 8 * 768], BF16)])
        for jp in range(J // 2):
            bkv = 2 + (jp % 2)
            rts = jp % 2
            P.op("pool", lambda e, jp=jp, rts=rts: e.dma_start(out=rt[rts], in_=rk[nm][2 * jp:2 * jp + 2].rearrange("t p c -> p t c")),
                 writes=[B_rt[rts]], dma=True, nbytes=65536)
            for tl in range(2):
                j = 2 * jp + tl
                xs_, hs = j % NXT, j % 2
                P.op("sp", lambda e, j=j, xs_=xs_: e.dma_start(out=xt[xs_], in_=x_str[j]), writes=[B_xt[xs_]], dma=True, nbytes=524288)
                ss, lnv, rstd = st_s[hs][:, 0:1], st_s[hs][:, 1:2], st_s[hs][:, 2:3]
                rms_rstd(xt[xs_], B_xt[xs_], junk, B_junk, ss, lnv, rstd, B_st[hs], D)
                P.op("dve", lambda e, xs_=xs_, hs=hs, rstd=rstd: e.tensor_scalar(hb[hs], xt[xs_], rstd, None, ALU.mult),
                     reads=[B_xt[xs_], B_st[hs]], writes=[B_hb[hs]], n=600)
                bt = j % 2
                psT = bank(bt).bitcast(BF16)
                for kc in range(8):
                    P.op("pe", lambda e, kc=kc, hs=hs, psT=psT: e.transpose(psT[:, kc * 128:(kc + 1) * 128], hb[hs][:, kc * 128:(kc + 1) * 128], ident),
                         reads=[B_hb[hs], B_ident], writes=[PB[bt]], n=128)
                P.op("dve", lambda e, hs=hs, psT=psT: e.tensor_copy(hT[hs].rearrange("p a b -> p (a b)"), psT),
                     reads=[PB[bt]], writes=[B_hT[hs]], n=700)
                bu = 4 + (j % 3)
                for kc in range(8):
                    P.op("pe", lambda e, kc=kc, hs=hs, tl=tl, bkv=bkv: e.matmul(bank(bkv)[:, tl * 256:(tl + 1) * 256], hT[hs][:, kc, :], w1[:, kc, 0:256],
                                                                       start=(kc == 0), stop=(kc == 7)),
                         reads=[B_hT[hs], B_w1], writes=[PB[bkv]], n=256)
                    P.op("pe", lambda e, kc=kc, hs=hs, bu=bu: e.matmul(bank(bu), hT[hs][:, kc, :], w1[:, kc, 256:768], start=(kc == 0), stop=(kc == 7)),
                         reads=[B_hT[hs], B_w1], writes=[PB[bu]], n=512)
                P.op("act", lambda e, j=j, bu=bu: e.activation(U[:, j, :], bank(bu), AF.Copy), reads=[PB[bu]], writes=[B_U], n=512)
                P.op("act", lambda e, j=j, tl=tl, bkv=bkv: e.activation(
                    vx[:, j, :].rearrange("p (a b) -> p a b", b=64)[:, 0:3:2, :],
                    bank(bkv)[:, tl * 256 + 128:(tl + 1) * 256].rearrange("p (a b) -> p a b", b=64), AF.Copy),
                    reads=[PB[bkv]], writes=[B_vx[j]], n=128)
            ksrc = bank(bkv).rearrange("p (t c) -> p t c", t=2)[:, :, 0:128]
            kslot = jp % 2
            normrope(ksrc, [PB[bkv]], 2, 2, gk, rt[rts], B_rt[rts], kb[kslot], B_kb[kslot], wkk[kslot])
            psK = bank(7).bitcast(BF16)
            for tl in range(2):
                P.op("pe", lambda e, tl=tl, kslot=kslot, psK=psK: e.transpose(psK[:, tl * 128:(tl + 1) * 128], kb[kslot][:, tl * 128:(tl + 1) * 128], ident),
                     reads=[B_kb[kslot], B_ident], writes=[PB[7]], n=128)
            P.op("dve", lambda e, jp=jp, psK=psK: e.tensor_copy(kT[:, jp * 256:(jp + 1) * 256], psK[:, 0:256]),
                 reads=[PB[7]], writes=[B_kT[2 * jp], B_kT[2 * jp + 1]], n=256)

        chk(nm + "_pass1", [("kT", kT, [128, S], BF16), ("vx", vx.rearrange("p a b -> p (a b)"), [128, J * 192], BF16),
                            ("U", U.rearrange("p a b -> p (a b)"), [128, 512 * J], BF16)])
        P.barrier()
        P.op("pool", lambda e: e.dma_start(out=gn, in_=gn_d[nm]), writes=[B_g], dma=True, nbytes=65536)
        P.op("dve", lambda e: e.tensor_copy(t1b, t1), reads=[B_tw], writes=[B_twb], n=256)
        P.op("dve", lambda e: e.tensor_copy(t2b, t2), reads=[B_tw], writes=[B_twb], n=256)
        NB = 128 // r
        PER2 = 512 // N2
        for g in range(4):
            xs = g % 2
            for bp in range(NB // 2):
                s1 = (g * (NB // 2) + bp) % NS1
                b1 = 0 + s1
                for bl in range(2):
                    cb = bp * 2 + bl
                    for cl in range(r):
                        cidx = g * 128 + cb * r + cl
                        P.op("pe", lambda e, bl=bl, cl=cl, cidx=cidx, b1=b1: e.matmul(bank(b1)[cl * J:(cl + 1) * J, bl * 256:(bl + 1) * 256],
                                                                             U.rearrange("p j c -> p (j c)")[:, cidx:J * 512:512], f1, start=True, stop=True,
                                                                             tile_position=(0, cl * J)),
                             reads=[B_U, B_f1], writes=[PB[b1]], n=256 // r + 40)
                pv = bank(b1).rearrange("p (b c) -> p b c", b=2)
                P.op("act", lambda e, s1=s1, pv=pv: e.activation(y1s[s1], pv, AF.Copy), reads=[PB[b1]], writes=[B_y1s[s1]], n=512)
                P.op("dve", lambda e, s1=s1: e.tensor_tensor(p12[s1][:, 0, :, :], y1s[s1], t1b.unsqueeze(1).to_broadcast([128, 2, 256]), ALU.mult),
                     reads=[B_y1s[s1], B_twb], writes=[B_p12[s1]], n=300)
                P.op("pool", lambda e, s1=s1: e.tensor_tensor(p12[s1][:, 1, 1, :], y1s[s1][:, 1, :], t2b, ALU.mult),
                     reads=[B_y1s[s1], B_twb], writes=[B_p2[s1]], n=256)
                P.op("dve", lambda e, s1=s1: e.tensor_tensor(p12[s1][:, 1, 0, :], y1s[s1][:, 0, :], t2b, ALU.mult),
                     reads=[B_y1s[s1], B_twb], writes=[B_p2b[s1]], n=200)
                for bl in range(2):
                    cb = bp * 2 + bl
                    grp2 = cb // PER2
                    pos2 = cb % PER2
                    b2 = 3 + (grp2 % 2)
                    terms = ((0, 0, ga), (1, 128, gn), (1, 0, gb), (0, 128, gb))
                    for ti, (wh, c0, gm) in enumerate(terms):
                        P.op("pe", lambda e, bl=bl, s1=s1, b2=b2, pos2=pos2, wh=wh, c0=c0, gm=gm, ti=ti: e.matmul(
                            bank(b2)[:, pos2 * N2:(pos2 + 1) * N2], p12[s1][:, wh, bl, c0:c0 + 128], gm, start=(ti == 0), stop=(ti == 3)),
                             reads=[B_p12[s1], B_p2[s1], B_p2b[s1], B_g], writes=[PB[b2]], n=N2)
                    if pos2 == PER2 - 1:
                        cb0 = grp2 * PER2
                        for ri in range(2):
                            src = bank(b2).rearrange("p (b ri k c) -> p b ri k c", b=PER2, ri=2, k=K2)[:, :, ri, :, :]
                            dst = xg[xs][:, :, ri, cb0 * r:(cb0 + PER2) * r].rearrange("p k (b c) -> p b k c", b=PER2)
                            P.op("act", lambda e, src=src, dst=dst: e.activation(dst, src, AF.Copy), reads=[PB[b2]], writes=[B_xg[xs]], n=256)
            for ch in range(NCH):
                s3 = ch % 2
                b3 = 5
                psT3 = bank(b3).bitcast(BF16).rearrange("p (ri t k) -> p ri t k", ri=2, t=4)
                for t in range(4):
                    for ri in range(2):
                        P.op("pe", lambda e, t=t, ri=ri, ch=ch, xs=xs, psT3=psT3: e.transpose(psT3[:, ri, t, :], xg[xs][:, ch * 4 + t, ri, :], ident),
                             reads=[B_xg[xs], B_ident], writes=[PB[b3]], n=128)
                P.op("act", lambda e, s3=s3, b3=b3: e.activation(xT3[s3].rearrange("p a b -> p (a b)"), bank(b3).bitcast(BF16), AF.Copy),
                     reads=[PB[b3]], writes=[B_xT3[s3]], n=1024)
                b4 = 6 + s3
                for ri in range(2):
                    P.op("pe", lambda e, ri=ri, s3=s3, b4=b4, g=g: e.matmul(bank(b4), qc[:, g, ri, :], xT3[s3][:, ri, :], start=(ri == 0), stop=(ri == 1)),
                         reads=[B_qc, B_xT3[s3]], writes=[PB[b4]], n=512)
                P.op("act", lambda e, b4=b4, g=g, ch=ch: e.activation(foT[:, g, ch * 512:(ch + 1) * 512], bank(b4), AF.Identity, bias=bfc[:, g:g + 1]),
                     reads=[PB[b4], B_small], writes=[B_fo[ch]], n=512)
        chk(nm + "_four", [("foT", foT.rearrange("p a b -> p (a b)"), [128, 4 * OWN], BF16),
                           ("xg1", xg[1].rearrange("p a b c -> p (a b c)"), [128, K2 * 256], BF16)])
        P.barrier()

        A2 = Alloc(JOB_END)
        w2 = A2.get((8, 1536), BF16)
        wo = A2.get((8, 1024), BF16)
        xt2 = [A2.get((1024,), F32) for _ in range(2)]
        hb2 = [A2.get((1024,), BF16) for _ in range(2)]
        hTc = A2.get((8, 512), BF16)
        junk2 = A2.get((1024,), BF16)
        st2 = [A2.get((4,), F32) for _ in range(4)]
        rt2 = [A2.get((2, 64), F32) for _ in range(2)]
        TP = 2 if nm == "p" else 1
        wkq = [wk_set(A2, 512 * TP)]
        qb = [A2.get((512 * TP,), BF16) for _ in range(2)]
        qT = [A2.get((4, 512), BF16) for _ in range(2)]
        sga = [A2.get((4, 512), BF16) for _ in range(2)]
        sgf = [A2.get((4, 512), BF16) for _ in range(2)]
        pT = [A2.get((1024,), BF16) for _ in range(3)]
        _yoff = A2.off
        yT = [A2.get((8, 512), BF16) for _ in range(2)]
        oab = A2.get((2, 512), F32)
        NEG = 1 if nm == "p" else 2
        eg = [A2.get((512,), F32) for _ in range(NEG)]
        B_eg = [Buf() for _ in range(NEG)]
        rc = A2.get((512,), F32)
        rr = [A2.get((1024,), F32) for _ in range(2)]
        _ay = Alloc(_yoff)
        stage2 = rr + [_ay.get((1024,), F32) for _ in range(4)]
        B_w2, B_wo, B_hTc, B_junk2, B_oab, B_rc = (Buf() for _ in range(6))
        B_xt2, B_hb2, B_rt2, B_qb, B_rr, B_qT, B_sga, B_sgf = ([Buf(), Buf()] for _ in range(8))
        B_st2 = [Buf() for _ in range(4)]
        B_stage2 = B_rr + [Buf() for _ in range(4)]
        B_pT = [Buf() for _ in range(3)]
        B_yT = [[Buf() for _ in range(8)] for _ in range(2)]

        nsl = [0]

        def nslot():
            nsl[0] += 1
            return nsl[0] % 6
        for c8 in range(8):
            sl = nslot()
            wq[0] += 1
            qn = "sp" if wq[0] % 2 else "pool"
            if c8 < 4:
                P.op(qn, lambda e, c8=c8, sl=sl: e.dma_start(out=stage2[sl][0:64, :], in_=w_out[64 * c8:64 * c8 + 64, :]),
                     writes=[B_stage2[sl]], dma=True, nbytes=262144)
                P.op(qn, lambda e, c8=c8, sl=sl: e.dma_start(out=stage2[sl][64:128, :], in_=w_out[256 + 64 * c8:256 + 64 * c8 + 64, :]),
                     writes=[B_stage2[sl]], dma=True, nbytes=262144)
            else:
                P.op(qn, lambda e, c8=c8, sl=sl: e.dma_start(out=stage2[sl], in_=w_out[512 + 128 * (c8 - 4):512 + 128 * (c8 - 3), :]),
                     writes=[B_stage2[sl]], dma=True, nbytes=524288)
            P.op("dve", lambda e, c8=c8, sl=sl: e.tensor_copy(wo[:, c8, :], stage2[sl]), reads=[B_stage2[sl]], writes=[B_wo], n=1024)
        for kc in range(8):
            for sec in [(0, 512, 0, True), (768, 512, 512, True), (1792, 512, 1024, False)]:
                sl = nslot()
                load_convert_win(w2, B_w2, kc, [sec], stage2[sl], B_stage2[sl])

        chk(nm + "_w2", [("w2", w2.rearrange("p a b -> p (a b)"), [128, 8 * 1536], BF16), ("wo", wo.rearrange("p a b -> p (a b)"), [128, 8 * 1024], BF16)])
        NKT = J
        cnt = dict(x=0, st=0, bg=0, step=0)

        def part_a(ch):
            P.strm = 1
            cs = ch % 2
            for tp in range(4 // TP):
                qs = (ch * (4 // TP) + tp) % 2
                tg0 = ch * 4 + tp * TP
                P.op("pool", lambda e, tg0=tg0, qs=qs: e.dma_start(out=rt2[qs][:, 0:TP, :], in_=rq[nm][tg0:tg0 + TP].rearrange("t p c -> p t c")),
                     writes=[B_rt2[qs]], dma=True, nbytes=32768 * TP)
                banks = []
                for tl in range(TP):
                    t = tp * TP + tl
                    tg = ch * 4 + t
                    xs_, hs = cnt["x"] % 2, tg % 2
                    cnt["x"] += 1
                    sts = cnt["st"] % 4
                    cnt["st"] += 1
                    P.op("sp", lambda e, tg=tg, xs_=xs_: e.dma_start(out=xt2[xs_], in_=x_nat[tg]), writes=[B_xt2[xs_]], dma=True, nbytes=524288)
                    ss, lnv, rstd = st2[sts][:, 0:1], st2[sts][:, 1:2], st2[sts][:, 2:3]
                    rms_rstd(xt2[xs_], B_xt2[xs_], junk2, B_junk2, ss, lnv, rstd, B_st2[sts], D, sq_eng="dve")
                    P.op("dve", lambda e, xs_=xs_, hs=hs, rstd=rstd: e.tensor_scalar(hb2[hs], xt2[xs_], rstd, None, ALU.mult),
                         reads=[B_xt2[xs_], B_st2[sts]], writes=[B_hb2[hs]], n=600)
                    if TP == 1:
                        ba, bb = 6 + (t % 2), 7 - (t % 2)
                    else:
                        ba = bb = 6 + tl
                    banks.append(bb)
                    psT = bank(ba).bitcast(BF16)
                    for kc in range(8):
                        P.op("pe", lambda e, kc=kc, hs=hs, psT=psT: e.transpose(psT[:, kc * 128:(kc + 1) * 128], hb2[hs][:, kc * 128:(kc + 1) * 128], ident),
                             reads=[B_hb2[hs], B_ident], writes=[PB[ba]], n=128)
                    P.op("dve", lambda e, t=t, psT=psT: e.tensor_copy(hTc[:, :, t * 128:(t + 1) * 128], psT.rearrange("p (a b) -> p a b", a=8)),
                         reads=[PB[ba]], writes=[B_hTc], n=700)
                    for kc in range(8):
                        P.op("pe", lambda e, kc=kc, t=t, bb=bb: e.matmul(bank(bb), hTc[:, kc, t * 128:(t + 1) * 128], w2[:, kc, 0:512], start=(kc == 0), stop=(kc == 7)),
                             reads=[B_hTc, B_w2], writes=[PB[bb]], n=512)
                if TP == 1:
                    qsrc, qsb = bank(banks[0]).rearrange("p (t c) -> p t c", t=1), [PB[banks[0]]]
                else:
                    qsrc, qsb = bank(6, 2).rearrange("p (t c) -> p t c", t=2), [PB[6], PB[7]]
                normrope(qsrc, qsb, TP, 8, gq, rt2[qs][:, 0:TP, :], B_rt2[qs], qb[qs], B_qb[qs], wkq[0], evac=True)
                bq = 6 + (tp % 2) if TP == 1 else 6
                psQ = bank(bq).bitcast(BF16)
                for tl in range(TP):
                    for hp in range(4):
                        P.op("pe", lambda e, hp=hp, tl=tl, qs=qs, psQ=psQ: e.transpose(psQ[:, (tl * 4 + hp) * 128:(tl * 4 + hp + 1) * 128],
                                                                                    qb[qs][:, (tl * 4 + hp) * 128:(tl * 4 + hp + 1) * 128], ident),
                             reads=[B_qb[qs], B_ident], writes=[PB[bq]], n=128)
                t0 = tp * TP
                P.op("dve", lambda e, t0=t0, psQ=psQ, cs=cs: e.tensor_copy(
                    qT[cs][:, :, t0 * 128:(t0 + TP) * 128].rearrange("p h (tl k) -> p tl h k", tl=TP),
                    psQ[:, 0:512 * TP].rearrange("p (tl h k) -> p tl h k", tl=TP, h=4)),
                    reads=[PB[bq]], writes=[B_qT[cs]], n=400 * TP)
            if ch == 0:
                chk(nm + "_t0", [("qT", qT[0].rearrange("p a b -> p (a b)"), [128, 2048], BF16), ("hTc", hTc.rearrange("p a b -> p (a b)"), [128, 4096], BF16)])
            for fc in range(8):
                bg = 6 + (cnt["bg"] % 2)
                cnt["bg"] += 1
                for kc in range(8):
                    P.op("pe", lambda e, kc=kc, fc=fc, bg=bg: e.matmul(bank(bg), w2[:, kc, 512 + fc * 128:512 + (fc + 1) * 128], hTc[:, kc, :],
                                                                   start=(kc == 0), stop=(kc == 7)),
                         reads=[B_hTc, B_w2], writes=[PB[bg]], n=512)
                es = cnt["bg"] % NEG
                dst = sga[cs][:, fc, :] if fc < 4 else sgf[cs][:, fc - 4, :]
                dbuf = B_sga[cs] if fc < 4 else B_sgf[cs]
                P.op("act", lambda e, bg=bg, es=es: e.activation(eg[es], bank(bg), AF.Exp, scale=-1.0), reads=[PB[bg]], writes=[B_eg[es]], n=512)
                P.op("dve", lambda e, es=es: e.tensor_scalar(eg[es], eg[es], 1.0, None, ALU.add), reads=[B_eg[es]], writes=[B_eg[es]], n=300)
                P.op("dve", lambda e, es=es: e.reciprocal(eg[es], eg[es]), reads=[B_eg[es]], writes=[B_eg[es]], n=3000)
                P.op("dve", lambda e, bg=bg, es=es, dst=dst: e.tensor_tensor(dst, bank(bg), eg[es], ALU.mult), reads=[PB[bg], B_eg[es]], writes=[dbuf], n=512)
                if fc >= 4:
                    P.op("pool", lambda e, fc=fc, ch=ch, cs=cs: e.tensor_tensor(yT[cs][:, fc, :], foT[:, fc - 4, ch * 512:(ch + 1) * 512], sgf[cs][:, fc - 4, :], ALU.mult),
                         reads=[B_fo[ch], B_sgf[cs], B_w2, B_wo], writes=[B_yT[cs][fc]], n=512)
            if ch == 0:
                chk(nm + "_A0", [("qT", qT[0].rearrange("p a b -> p (a b)"), [128, 2048], BF16), ("sga", sga[0].rearrange("p a b -> p (a b)"), [128, 2048], BF16),
                                 ("sgf", sgf[0].rearrange("p a b -> p (a b)"), [128, 2048], BF16), ("yT", yT[0].rearrange("p a b -> p (a b)"), [128, 4096], BF16)])

        def part_b(ch):
            P.strm = 0
            cs = ch % 2
            steps = [(hp, kt) for hp in range(4) for kt in range(NKT)]

            def qk(i):
                hp, kt = steps[i]
                sb = 2 * (i % 2)
                P.op("pe", lambda e, kt=kt, sb=sb, hp=hp: e.matmul(bank(sb), kT[0:64, kt * 128:(kt + 1) * 128], qT[cs][0:64, hp, :], start=True, stop=True),
                     reads=[B_kT[kt], B_qT[cs]], writes=[PB[sb]], n=300)
                P.op("pe", lambda e, kt=kt, sb=sb, hp=hp: e.matmul(bank(sb + 1), kT[64:128, kt * 128:(kt + 1) * 128], qT[cs][64:128, hp, :], start=True, stop=True),
                     reads=[B_kT[kt], B_qT[cs]], writes=[PB[sb + 1]], n=300)

            def ex(i):
                sb = 2 * (i % 2)
                ps = i % 3
                P.op("act", lambda e, sb=sb, ps=ps: e.activation(pT[ps], bank(sb, 2), AF.Exp, scale=0.125),
                     reads=[PB[sb], PB[sb + 1]], writes=[B_pT[ps]], n=1024)

            def pv(i):
                hp, kt = steps[i]
                ps = i % 3
                P.op("pe", lambda e, kt=kt, ps=ps: e.matmul(bank(4), vx[:, kt, 0:128], pT[ps][:, 0:512], start=(kt == 0), stop=(kt == NKT - 1)),
                     reads=[B_vx[kt], B_pT[ps]], writes=[PB[4]], n=512)
                P.op("pe", lambda e, kt=kt, ps=ps: e.matmul(bank(5), vx[:, kt, 64:192], pT[ps][:, 512:1024], start=(kt == 0), stop=(kt == NKT - 1)),
                     reads=[B_vx[kt], B_pT[ps]], writes=[PB[5]], n=512)

            def finalize(hp):
                P.op("dve", lambda e: e.tensor_copy(oab[:, 0, :], bank(4)), reads=[PB[4]], writes=[B_oab], n=512)
                P.op("dve", lambda e: e.tensor_copy(oab[:, 1, :], bank(5)), reads=[PB[5]], writes=[B_oab], n=512)
                P.op("dve", lambda e: e.tensor_copy(rc[0:64, :], oab[64:128, 0, :]), reads=[B_oab], writes=[B_rc], n=256)
                P.op("dve", lambda e: e.tensor_copy(rc[64:128, :], oab[0:64, 1, :]), reads=[B_oab], writes=[B_rc], n=256)
                P.op("dve", lambda e: e.reciprocal(rc, rc), reads=[B_rc], writes=[B_rc], n=3000)
                P.op("pool", lambda e, hp=hp: e.tensor_tensor(rc, rc, sga[cs][:, hp, :], ALU.mult), reads=[B_rc, B_sga[cs]], writes=[B_rc], n=512)
                P.op("dve", lambda e, hp=hp: e.tensor_tensor(yT[cs][0:64, hp, :], oab[0:64, 0, :], rc[0:64, :], ALU.mult),
                     reads=[B_oab, B_rc, B_w2, B_wo], writes=[B_yT[cs][hp]], n=512)
                P.op("dve", lambda e, hp=hp: e.tensor_tensor(yT[cs][64:128, hp, :], oab[64:128, 1, :], rc[64:128, :], ALU.mult),
                     reads=[B_oab, B_rc, B_w2, B_wo], writes=[B_yT[cs][hp]], n=512)

            NS = len(steps)
            qk(0)
            qk(1)
            for i in range(NS):
                ex(i)
                if i + 2 < NS:
                    qk(i + 2)
                pv(i)
                if steps[i][1] == NKT - 1:
                    finalize(steps[i][0])
            if ch == 0:
                chk(nm + "_B0", [("yT", yT[0].rearrange("p a b -> p (a b)"), [128, 4096], BF16)])

        def part_c(ch):
            P.strm = 1
            cs = ch % 2
            for t in range(4):
                tg = ch * 4 + t
                xs_, os_ = cnt["x"] % 2, tg % 2
                cnt["x"] += 1
                sts = cnt["st"] % 4
                cnt["st"] += 1
                P.op("sp", lambda e, tg=tg, xs_=xs_: e.dma_start(out=xt2[xs_], in_=x_nat[tg]), writes=[B_xt2[xs_]], dma=True, nbytes=524288)
                for half in range(2):
                    for c8 in range(8):
                        P.op("pe", lambda e, c8=c8, half=half, t=t: e.matmul(bank(6 + half), yT[cs][:, c8, t * 128:(t + 1) * 128], wo[:, c8, half * 512:(half + 1) * 512],
                                                                     start=(c8 == 0), stop=(c8 == 7)),
                             reads=[B_yT[cs][c8], B_wo], writes=[PB[6 + half]], n=512)
                P.op("dve", lambda e, xs_=xs_, os_=os_: e.tensor_tensor(rr[os_], bank(6, 2), xt2[xs_], ALU.add),
                     reads=[PB[6], PB[7], B_xt2[xs_], B_w2, B_wo], writes=[B_rr[os_]], n=1024)
                ss, lnv, rstd = st2[sts][:, 0:1], st2[sts][:, 1:2], st2[sts][:, 2:3]
                rms_rstd(rr[os_], B_rr[os_], junk2, B_junk2, ss, lnv, rstd, B_st2[sts], D, sq_eng="dve")
                P.op("dve", lambda e, os_=os_, rstd=rstd: e.scalar_tensor_tensor(rr[os_], rr[os_], rstd, fn_bc, ALU.mult, ALU.mult),
                     reads=[B_rr[os_], B_st2[sts], B_fn], writes=[B_rr[os_]], n=1024)
                P.op("sp", lambda e, tg=tg, os_=os_: e.dma_start(out=y_nat[tg], in_=rr[os_]), reads=[B_rr[os_]], dma=True, nbytes=524288)

        P.group_begin(); part_a(0); P.group_end(-1.0, -0.01)
        for ch in range(NCH):
            if ch + 1 < NCH:
                P.group_begin(); part_a(ch + 1); P.group_end(ch + 0.0, ch + 0.60)
            P.group_begin(); part_b(ch); P.group_end(float(ch), ch + 1.0)
            if ch + 1 < NCH:
                P.group_begin(); part_c(ch); P.group_end(ch + 1.0, ch + 1.5)
            else:
                P.group_begin(); part_c(ch); P.group_end(float(NCH), NCH + 1.0)
        P.strm = 0
        P.barrier()

    try:
        for job in JOBS:
            if not stopped:
                run_job(job)
    except _Stop:
        pass

    P.schedule()
    P.emit()
    st.close()
    return nc


_NC = None
_CONST = None


def _consts():
    global _CONST
    if _CONST is not None:
        return _CONST
    c = {}
    p = np.arange(128)
    c["rk_s"] = np.stack([rope_tab(64 * p + j) for j in range(64)]).astype(np.float32)
    c["rk_p"] = np.stack([rope_tab(32 * p + j) for j in range(32)]).astype(np.float32)
    c["rq_p"] = np.stack([rope_tab(128 * k2 + p) for k2 in range(32)]).astype(np.float32)
    c["rq_s"] = [np.stack([rope_tab(2048 * qr + 128 * k2 + p) for k2 in range(16)]).astype(np.float32) for qr in range(4)]
    c["f1"] = f1_mat().astype(np.float32).astype(bf)
    c["t1_s"], c["t2_s"] = twiddle(64)
    c["t1_p"], c["t2_p"] = twiddle(32)
    gs = [g_mats(64, list(range(16 * qr, 16 * qr + 16))) for qr in range(4)]
    c["ga_s"] = [g[0].astype(np.float32).astype(bf) for g in gs]
    c["gb_s"] = [g[1].astype(np.float32).astype(bf) for g in gs]
    c["gn_s"] = [(-g[0]).astype(np.float32).astype(bf) for g in gs]
    gp = g_mats(32, list(range(32)))
    c["ga_p"] = gp[0].astype(np.float32).astype(bf)
    c["gb_p"] = gp[1].astype(np.float32).astype(bf)
    c["gn_p"] = (-gp[0]).astype(np.float32).astype(bf)
    c["cc"], c["sc"] = cs_mats()
    _CONST = c
    return c


def kernel(x_prompt, x_sample, ln_w, w_in, q_norm, k_norm, w_fourier, b_fourier, w_out, final_norm):
    global _NC
    if _NC is None:
        _NC = build()
    nc = _NC
    c = _consts()
    f32 = np.float32
    x_prompt = np.asarray(x_prompt, f32)
    x_sample = np.asarray(x_sample, f32)
    shared = {
        "ln_w": np.ascontiguousarray(np.asarray(ln_w, f32)[0]),
        "w_in": np.ascontiguousarray(np.asarray(w_in, f32)[0]),
        "q_norm": np.ascontiguousarray(np.asarray(q_norm, f32)[0]),
        "k_norm": np.ascontiguousarray(np.asarray(k_norm, f32)[0]),
        "w_f": np.ascontiguousarray(np.asarray(w_fourier, f32)[0]),
        "b_f": np.ascontiguousarray(np.asarray(b_fourier, f32)[0]),
        "w_out": np.ascontiguousarray(np.asarray(w_out, f32)[0]),
        "fnorm": np.ascontiguousarray(np.asarray(final_norm, f32)),
        "rk_s": c["rk_s"], "rk_p": c["rk_p"], "rq_p": c["rq_p"], "f1": c["f1"],
        "t1_s": c["t1_s"], "t2_s": c["t2_s"], "t1_p": c["t1_p"], "t2_p": c["t2_p"],
        "ga_p": c["ga_p"], "gb_p": c["gb_p"], "gn_p": c["gn_p"], "cc": c["cc"], "sc": c["sc"],
    }
    in_maps = []
    for core in range(N_CORES):
        b, qr = core // 4, core % 4
        m = dict(shared)
        m["xp"] = np.ascontiguousarray(x_prompt[core])
        m["xs"] = np.ascontiguousarray(x_sample[b])
        m["xo"] = np.ascontiguousarray(x_sample[b, 2048 * qr:2048 * (qr + 1)])
        m["rq_s"] = c["rq_s"][qr]
        m["ga_s"] = c["ga_s"][qr]
        m["gb_s"] = c["gb_s"][qr]
        m["gn_s"] = c["gn_s"][qr]
        in_maps.append(m)
    res = run_bass_kernel_spmd(nc, in_maps, core_ids=list(range(N_CORES)))
    y_prompt = np.stack([np.asarray(res.results[core]["yp"], f32) for core in range(N_CORES)])
    y_sample = np.empty((2, 8192, D), f32)
    for core in range(N_CORES):
        b, qr = core // 4, core % 4
        y_sample[b, 2048 * qr:2048 * (qr + 1)] = np.asarray(res.results[core]["yo"], f32)
    return (y_prompt, y_sample)
```

```python
import contextlib
import numpy as np
import ml_dtypes
import concourse.bass as bass
import concourse.mybir as mybir
from concourse.bass_utils import run_bass_kernel_spmd

F32 = mybir.dt.float32
BF16 = mybir.dt.bfloat16
AF = mybir.ActivationFunctionType
ALU = mybir.AluOpType
AX = mybir.AxisListType
bf = ml_dtypes.bfloat16

D = 1024
INW = 2304
EPS = 1e-6
N_CORES = 8


class Buf:
    __slots__ = ("name", "ws", "rs", "excl")

    def __init__(self, name="", excl=False):
        self.name = name
        self.ws = []
        self.rs = []
        self.excl = excl


class Op:
    __slots__ = ("eng", "fn", "deps", "ords", "sig", "cnt", "dma", "dsem", "dcnt", "c", "idx", "seg", "strm",
                 "t0", "t1", "nbytes", "done", "prio")

    def __init__(self, eng, fn, dma):
        self.eng = eng
        self.fn = fn
        self.deps = []
        self.ords = []
        self.sig = False
        self.cnt = 0
        self.dma = dma
        self.dsem = None
        self.dcnt = 0
        self.done = False


ENGS = ("pe", "act", "dve", "pool", "sp")


def _cost(eng, dma, n, nbytes):
    if dma:
        return 0.06
    if eng == "pe":
        return n / 2400.0 + 0.03
    if eng == "act":
        return (n + 150) / 1200.0
    if eng == "dve":
        return (n + 150) / 960.0
    return n * 0.002 + 0.2


class Prog:
    def __init__(self, nc, n_dma_sems=8):
        self.nc = nc
        self.ops = {e: [] for e in ENGS}
        self.n_dma_sems = n_dma_sems
        self.dma_rr = {e: 0 for e in ENGS}
        self.dma_last = {}
        self.seg = 0
        self.strm = 0
        self.nops = 0
        self.grp = None

    def op(self, eng, fn, reads=(), writes=(), dma=False, n=128, nbytes=0):
        o = Op(eng, fn, dma)
        o.c = _cost(eng, dma, n, nbytes)
        o.nbytes = nbytes
        o.idx = self.nops
        self.nops += 1
        o.seg = self.seg
        o.strm = self.strm
        o.prio = 0.0
        if self.grp is not None:
            self.grp.append(o)
        deps = o.deps
        excl_reads = [b for b in reads if b.excl]
        if excl_reads:
            reads = [b for b in reads if not b.excl]
            writes = list(writes) + [b for b in excl_reads if b not in writes]
        for b in reads:
            for w in b.ws:
                if w is not o:
                    deps.append(w)
            b.rs.append(o)
        for b in writes:
            for w in b.ws:
                if w is o:
                    continue
                if eng == "pe" and w.eng == "pe" and not w.dma and not dma:
                    o.ords.append(w)
                else:
                    deps.append(w)
            for r in b.rs:
                if r is not o:
                    deps.append(r)
            b.ws = [o]
            b.rs = []
        if dma:
            slot = self.dma_rr[eng]
            self.dma_rr[eng] = (slot + 1) % self.n_dma_sems
            prev = self.dma_last.get((eng, slot))
            if prev is not None:
                deps.append(prev)
            self.dma_last[(eng, slot)] = o
            o.dsem = (eng, slot)
        self.ops[eng].append(o)
        return o

    def group_begin(self):
        self.grp = []

    def group_end(self, p0, p1):
        g, self.grp = self.grp, None
        n = max(len(g), 1)
        for i, o in enumerate(g):
            o.prio = p0 + (p1 - p0) * i / n

    def barrier(self):
        self.seg += 1

    def schedule(self, W=96):
        HOP = 0.5
        BG_SLACK = 2.0
        self.seg_span = []
        allops = [o for e in ENGS for o in self.ops[e]]
        nseg = self.seg + 1
        new = {e: [] for e in ENGS}
        for sg in range(nseg):
            qs = {}
            for e in ENGS:
                for o in self.ops[e]:
                    if o.seg == sg:
                        qs.setdefault((e, o.strm), []).append(o)
            heads = {k: 0 for k in qs}
            free = {e: 0.0 for e in ENGS}
            dma_free = 0.0
            remaining = sum(len(v) for v in qs.values())
            out = {e: [] for e in ENGS}
            while remaining:
                best = None
                for (e, sid), lst in qs.items():
                    h = heads[(e, sid)]
                    while h < len(lst) and lst[h].done:
                        h += 1
                    heads[(e, sid)] = h
                    cnt = 0
                    i = h
                    fe = free[e]
                    while i < len(lst) and cnt < W:
                        o = lst[i]
                        i += 1
                        if o.done:
                            continue
                        cnt += 1
                        rdy = 0.0
                        ok = True
                        for d in o.deps:
                            if d.seg != sg:
                                continue
                            if not d.done:
                                ok = False
                                break
                            t = d.t1 + (HOP if d.eng != e or d.dma else 0.02)
                            if t > rdy:
                                rdy = t
                        if ok:
                            for d in o.ords:
                                if d.seg == sg and not d.done:
                                    ok = False
                                    break
                        if not ok:
                            continue
                        if sid == 1 and e == "pe":
                            rdy += BG_SLACK
                        stt = rdy if rdy > fe else fe
                        key = (stt, o.prio, o.idx)
                        if best is None or key < best[0]:
                            best = (key, o)
                        if stt <= fe:
                            break
                assert best is not None, "scheduler stuck (dependency cycle?)"
                (stt, _, _), o = best
                o.t0 = stt
                if o.dma:
                    xs = max(stt, dma_free)
                    dma_free = xs + o.nbytes / 150000.0
                    o.t1 = dma_free + 2.0
                    free[o.eng] = stt + o.c
                else:
                    o.t1 = stt + o.c
                    free[o.eng] = o.t1
                o.done = True
                out[o.eng].append(o)
                remaining -= 1
            for e in ENGS:
                new[e].extend(out[e])
            self.seg_span.append(max([o.t1 for e in ENGS for o in out[e]] + [0.0]))
        self.ops = new

    def emit(self):
        nc = self.nc
        nseg = self.seg + 1
        lastc = [dict() for _ in range(nseg)]
        lastd = [dict() for _ in range(nseg)]
        for e in ENGS:
            for o in self.ops[e]:
                if o.dma:
                    lastd[o.seg][o.dsem] = o
                else:
                    lastc[o.seg][e] = o
        for e in ENGS:
            cur = -1
            for o in self.ops[e]:
                if o.seg != cur:
                    for sg in range(max(cur, 0), o.seg):
                        o.deps.extend(lastc[sg].values())
                        o.deps.extend(lastd[sg].values())
                    cur = o.seg
        for e in ENGS:
            for o in self.ops[e]:
                for d in o.deps:
                    d.sig = True
        with contextlib.ExitStack() as st:
            esem = {e: st.enter_context(nc.semaphore("c_" + e)) for e in ENGS}
            dsem = {}
            for e in ENGS:
                if any(o.dma for o in self.ops[e]):
                    for s in range(self.n_dma_sems):
                        dsem[(e, s)] = st.enter_context(nc.semaphore("d_%s%d" % (e, s)))
            dcount = {k: 0 for k in dsem}
            for e in ENGS:
                c = 0
                for o in self.ops[e]:
                    if o.dma:
                        dcount[o.dsem] += 16
                        o.dcnt = dcount[o.dsem]
                    elif o.sig:
                        c += 1
                        o.cnt = c
            block = st.enter_context(nc.Block())
            engobj = {"pe": block.tensor, "act": block.scalar, "dve": block.vector,
                      "pool": block.gpsimd, "sp": block.sync}

            def make(e):
                ops = self.ops[e]

                def body(eng):
                    seen = {}
                    for o in ops:
                        need = {}
                        for d in o.deps:
                            if d.dma:
                                k = ("d",) + d.dsem
                                v = d.dcnt
                                sem = dsem[d.dsem]
                            else:
                                k = ("c", d.eng)
                                v = d.cnt
                                sem = esem[d.eng]
                            if seen.get(k, 0) >= v:
                                continue
                            if k not in need or need[k][1] < v:
                                need[k] = (sem, v)
                        for k, (sem, v) in need.items():
                            eng.wait_ge(sem, v)
                            seen[k] = v
                        ins = o.fn(eng)
                        if o.dma:
                            ins.then_inc(dsem[o.dsem], 16)
                        elif o.sig:
                            ins.then_inc(esem[e], 1)
                    if e == "sp":
                        for k, sem in dsem.items():
                            if dcount[k] > 0:
                                eng.wait_ge(sem, dcount[k])
                return body

            for e in ENGS:
                if self.ops[e] or e == "sp":
                    engobj[e](make(e))


def rope_tab(tok):
    tok = np.asarray(tok)
    row = (tok // 64).astype(np.float32)
    col = (tok % 64).astype(np.float32)
    inv = (np.float32(10000.0) ** (-np.arange(0, 32, 2, dtype=np.float32) / np.float32(32))).astype(np.float32)
    ang = np.concatenate([row[..., None] * inv, col[..., None] * inv], axis=-1).astype(np.float32)
    return np.concatenate([np.cos(ang), np.sin(ang)], axis=-1).astype(np.float32)


def f1_mat():
    p = np.arange(128)[:, None]
    k = np.arange(128)[None, :]
    a = 2 * np.pi * p * k / 128.0
    return np.concatenate([np.cos(a), -np.sin(a)], axis=1) / np.sqrt(128.0)


def twiddle(J):
    S = 128 * J
    j = (np.arange(128) % J)[:, None]
    k1 = np.arange(128)[None, :]
    a = 2 * np.pi * j * k1 / S
    tr, ti = np.cos(a), -np.sin(a)
    return np.concatenate([tr, tr], 1).astype(np.float32), np.concatenate([ti, ti], 1).astype(np.float32)


def g_mats(J, k2list):
    r = 128 // J
    K2 = len(k2list)
    m = np.arange(128)
    cl = m // J
    j = m % J
    n1 = K2 * r
    Gr = np.zeros((128, n1))
    Gi = np.zeros((128, n1))
    for k2i, k2 in enumerate(k2list):
        for c in range(r):
            n = k2i * r + c
            sel = cl == c
            a = 2 * np.pi * j[sel] * k2 / J
            Gr[sel, n] = np.cos(a) / np.sqrt(J)
            Gi[sel, n] = -np.sin(a) / np.sqrt(J)
    return np.concatenate([Gr, Gi], 1), np.concatenate([-Gi, Gr], 1)


def cs_mats():
    c = np.arange(128)
    a = 2 * np.pi * np.outer(c, c) / 128.0
    return (np.cos(a) / np.sqrt(128.0)).astype(np.float32), (np.sin(a) / np.sqrt(128.0)).astype(np.float32)


JOBS = (
    dict(nm="s", S=8192, J=64, own=2048),
    dict(nm="p", S=4096, J=32, own=4096),
)


class _Stop(Exception):
    pass


def build(stop=None):
    nc = bass.Bass("TRN2", target_bir_lowering=False)
    dbg_out = {}

    def chk(name, dumps):
        if stop != name:
            return
        P.barrier()
        for i, (dn, ap, shape, dt) in enumerate(dumps):
            d = nc.dram_tensor("dbg_" + dn, list(shape), dt, kind="ExternalOutput").ap()
            dbg_out[dn] = d
            P.op("sp", lambda e, d=d, ap=ap: e.dma_start(out=d, in_=ap), dma=True)
        raise _Stop()

    def din(name, shape, dt=F32):
        return nc.dram_tensor(name, list(shape), dt, kind="ExternalInput").ap()

    xfull = {"s": din("xs", [8192, D]), "p": din("xp", [4096, D])}
    xown = {"s": din("xo", [2048, D]), "p": xfull["p"]}
    ln_w = din("ln_w", [D])
    w_in = din("w_in", [D, INW])
    q_norm = din("q_norm", [64])
    k_norm = din("k_norm", [64])
    w_f = din("w_f", [4, 128, 128])
    b_f = din("b_f", [4, 128])
    w_out = din("w_out", [D, D])
    fnorm = din("fnorm", [D])
    rk = {"s": din("rk_s", [64, 128, 64]), "p": din("rk_p", [32, 128, 64])}
    rq = {"s": din("rq_s", [16, 128, 64]), "p": din("rq_p", [32, 128, 64])}
    f1_d = din("f1", [128, 256], BF16)
    t1_d = {"s": din("t1_s", [128, 256]), "p": din("t1_p", [128, 256])}
    t2_d = {"s": din("t2_s", [128, 256]), "p": din("t2_p", [128, 256])}
    ga_d = {"s": din("ga_s", [128, 64], BF16), "p": din("ga_p", [128, 256], BF16)}
    gb_d = {"s": din("gb_s", [128, 64], BF16), "p": din("gb_p", [128, 256], BF16)}
    gn_d = {"s": din("gn_s", [128, 64], BF16), "p": din("gn_p", [128, 256], BF16)}
    cc_d = din("cc", [128, 128])
    sc_d = din("sc", [128, 128])
    yout = {"s": nc.dram_tensor("yo", [2048, D], F32, kind="ExternalOutput").ap(),
            "p": nc.dram_tensor("yp", [4096, D], F32, kind="ExternalOutput").ap()}

    ARENA_BYTES = 207 * 1024
    st = contextlib.ExitStack()
    arena = st.enter_context(nc.sbuf_tensor("arena", [128, ARENA_BYTES // 4], F32))
    psum = st.enter_context(nc.psum_tensor("psum", [128, 4096], F32))
    P = Prog(nc)

    class Alloc:
        def __init__(self, base=0):
            self.off = base

        def get(self, free_shape, dt):
            nel = int(np.prod(free_shape))
            sz = nel * (4 if dt == F32 else 2)
            sz = (sz + 63) // 64 * 64
            a = arena[:, self.off // 4:(self.off + sz) // 4]
            self.off += sz
            assert self.off <= ARENA_BYTES, ("SBUF arena overflow", self.off)
            if dt != F32:
                a = a.bitcast(dt)
            a = a[:, 0:nel]
            if len(free_shape) == 2:
                a = a.rearrange("p (a b) -> p a b", b=free_shape[1])
            elif len(free_shape) == 3:
                a = a.rearrange("p (a b c) -> p a b c", b=free_shape[1], c=free_shape[2])
            return a

    def bank(i, n=1):
        return psum[:, 512 * i:512 * (i + n)]

    PB = [Buf("pb%d" % i, excl=True) for i in range(8)]

    A0 = Alloc(0)
    ident = A0.get((128,), BF16)
    f1 = A0.get((256,), BF16)
    lnw_col = A0.get((8,), F32)
    gq = A0.get((64,), F32)
    gk = A0.get((64,), F32)
    bfc = A0.get((4,), F32)
    fn_bc = A0.get((1024,), F32)
    qc = A0.get((4, 2, 128), BF16)
    cc_t = A0.get((128,), F32)
    sc_t = A0.get((128,), F32)
    epsc = A0.get((1,), F32)
    wf_t = A0.get((4, 128), F32)
    neg1 = A0.get((1,), F32)
    B_ident, B_f1, B_small, B_fn, B_qc, B_cs = Buf(), Buf(), Buf(), Buf(), Buf(), Buf()
    PERSIST_END = A0.off

    P.op("pool", lambda e: e.memset(ident, 1.0), writes=[B_ident])
    P.op("pool", lambda e: e.affine_select(ident, ident, [[-1, 128]], ALU.is_equal, 0.0, base=0, channel_multiplier=1),
         reads=[B_ident], writes=[B_ident])
    P.op("pool", lambda e: e.memset(epsc, EPS), writes=[B_small])
    P.op("pool", lambda e: e.memset(neg1, -1.0), writes=[B_small])
    P.op("pool", lambda e: e.dma_start(out=f1, in_=f1_d), writes=[B_f1], dma=True)
    P.op("pool", lambda e: e.dma_start(out=lnw_col, in_=ln_w.rearrange("(k p) -> p k", p=128), allow_slow_non_contiguous=True),
         writes=[B_small], dma=True)
    P.op("pool", lambda e: e.dma_start(out=gq, in_=q_norm.partition_broadcast(128)), writes=[B_small], dma=True)
    P.op("pool", lambda e: e.dma_start(out=gk, in_=k_norm.partition_broadcast(128)), writes=[B_small], dma=True)
    P.op("pool", lambda e: e.dma_start(out=bfc, in_=b_f.rearrange("g d -> d g"), allow_slow_non_contiguous=True),
         writes=[B_small], dma=True)
    P.op("pool", lambda e: e.dma_start(out=fn_bc, in_=fnorm.partition_broadcast(128)), writes=[B_fn], dma=True)
    P.op("pool", lambda e: e.dma_start(out=cc_t, in_=cc_d), writes=[B_cs], dma=True)
    P.op("pool", lambda e: e.dma_start(out=sc_t, in_=sc_d), writes=[B_cs], dma=True)

    B_wf = Buf()
    P.op("pool", lambda e: e.dma_start(out=wf_t, in_=w_f.rearrange("g c d -> c g d")), writes=[B_wf], dma=True)
    for g in range(4):
        for ri, cst in enumerate((cc_t, sc_t)):
            bk = (g * 2 + ri) % 8
            P.op("pe", lambda e, cst=cst, g=g, bk=bk: e.matmul(bank(bk)[:, 0:128], cst, wf_t[:, g, :], start=True, stop=True),
                 reads=[B_cs, B_wf], writes=[PB[bk]])
            P.op("dve", lambda e, g=g, ri=ri, bk=bk: e.tensor_copy(qc[:, g, ri, :], bank(bk)[:, 0:128]),
                 reads=[PB[bk]], writes=[B_qc])
    stopped = False
    try:
        chk("setup", [("qc", qc.rearrange("p a b c -> p (a b c)"), [128, 1024], BF16), ("ident", ident, [128, 128], BF16),
                      ("lnw", lnw_col, [128, 8], F32), ("gq", gq, [128, 64], F32), ("bfc", bfc, [128, 4], F32), ("fn", fn_bc, [128, 1024], F32)])
    except _Stop:
        stopped = True

    def rms_rstd(x_ap, x_buf, junk, junk_buf, ss, lnv, rstd, sbuf, ncols, sq_eng="act"):
        if sq_eng == "act":
            P.op("act", lambda e: e.activation(junk, x_ap, AF.Square, accum_out=ss), reads=[x_buf], writes=[junk_buf, sbuf], n=ncols)
        else:
            P.op("dve", lambda e: e.tensor_tensor(junk, x_ap, x_ap, ALU.mult), reads=[x_buf], writes=[junk_buf], n=ncols)
            P.op("dve", lambda e: e.tensor_reduce(ss, junk, AX.X, ALU.add), reads=[junk_buf], writes=[sbuf], n=ncols)
        P.op("act", lambda e: e.activation(lnv, ss, AF.Ln, bias=epsc, scale=1.0 / ncols), reads=[sbuf, B_small], writes=[sbuf], n=1)
        P.op("act", lambda e: e.activation(rstd, lnv, AF.Exp, scale=-0.5), reads=[sbuf], writes=[sbuf], n=1)

    def normrope(src, src_bufs, T, H, gains, tab, tab_buf, outb, out_buf, wk, reng="pool", evac=False):
        N = T * H * 64
        xf, sq, ta, tb = wk["xf"][:, 0:N], wk["sq"][:, 0:N], wk["ta"][:, 0:N // 2], wk["tb"][:, 0:N // 2]
        ssq, lnq, rsq = wk["ssq"][:, 0:T * H], wk["lnq"][:, 0:T * H], wk["rsq"][:, 0:T * H]
        BW = wk["buf"]
        xf3 = xf.rearrange("p (t c) -> p t c", t=T)
        sq3 = sq.rearrange("p (t c) -> p t c", t=T)
        if evac:
            P.op("dve", lambda e, psrc=src: e.tensor_copy(xf3, psrc), reads=src_bufs, writes=[BW["xf"]], n=N)
            src, src_bufs = xf3, [BW["xf"]]
        if evac:
            P.op("dve", lambda e: e.tensor_tensor(sq3, src, src, ALU.mult), reads=src_bufs, writes=[BW["sq"]], n=N)
        else:
            P.op("act", lambda e: e.activation(sq3, src, AF.Square), reads=src_bufs, writes=[BW["sq"]], n=N)
        P.op("dve", lambda e: e.tensor_reduce(ssq, sq.rearrange("p (h d) -> p h d", d=64), AX.X, ALU.add),
             reads=[BW["sq"]], writes=[BW["st"]], n=N)
        P.op("act", lambda e: e.activation(lnq, ssq, AF.Ln, bias=epsc, scale=1.0 / 64), reads=[BW["st"], B_small], writes=[BW["st"]], n=8)
        P.op("act", lambda e: e.activation(rsq, lnq, AF.Exp, scale=-0.5), reads=[BW["st"]], writes=[BW["st"]], n=8)
        P.op("dve", lambda e: e.tensor_tensor(sq3.rearrange("p t (h d) -> p t h d", d=64), src.rearrange("p t (h d) -> p t h d", d=64),
                                              rsq.rearrange("p (t h) -> p t h", t=T).unsqueeze(3).to_broadcast([128, T, H, 64]), ALU.mult),
             reads=src_bufs + [BW["st"]], writes=[BW["sq"]], n=N)
        P.op("dve", lambda e: e.tensor_tensor(xf.rearrange("p (h d) -> p h d", d=64), sq.rearrange("p (h d) -> p h d", d=64),
                                              gains.unsqueeze(1).to_broadcast([128, T * H, 64]), ALU.mult),
             reads=[BW["sq"], B_small], writes=[BW["xf"]], n=N)
        x4 = xf.rearrange("p (t h i two) -> p t h i two", t=T, h=H, two=2)
        o4 = outb.rearrange("p (t h i two) -> p t h i two", t=T, h=H, two=2)
        x1, x2 = x4[:, :, :, :, 0], x4[:, :, :, :, 1]
        cosb = tab[:, :, 0:32].unsqueeze(2).to_broadcast([128, T, H, 32])
        sinb = tab[:, :, 32:64].unsqueeze(2).to_broadcast([128, T, H, 32])
        ta4 = ta.rearrange("p (t h i) -> p t h i", t=T, h=H)
        tb4 = tb.rearrange("p (t h i) -> p t h i", t=T, h=H)
        h2 = N // 2
        P.op(reng, lambda e: e.tensor_tensor(ta4, x1, cosb, ALU.mult), reads=[BW["xf"], tab_buf], writes=[BW["ta"]], n=h2)
        P.op(reng, lambda e: e.tensor_tensor(tb4, x2, sinb, ALU.mult), reads=[BW["xf"], tab_buf], writes=[BW["tb"]], n=h2)
        P.op(reng, lambda e: e.tensor_tensor(o4[:, :, :, :, 0], ta4, tb4, ALU.subtract), reads=[BW["ta"], BW["tb"]], writes=[out_buf], n=h2)
        P.op(reng, lambda e: e.tensor_tensor(ta4, x1, sinb, ALU.mult), reads=[BW["xf"], tab_buf], writes=[BW["ta"]], n=h2)
        P.op(reng, lambda e: e.tensor_tensor(tb4, x2, cosb, ALU.mult), reads=[BW["xf"], tab_buf], writes=[BW["tb"]], n=h2)
        P.op(reng, lambda e: e.tensor_tensor(o4[:, :, :, :, 1], ta4, tb4, ALU.add), reads=[BW["ta"], BW["tb"]], writes=[out_buf], n=h2)

    wq = [0]

    def load_convert_win(dst, dst_buf, kc, sections, stage, stage_buf):
        for (c0, n, d0, perm) in sections:
            wq[0] += 1
            P.op("sp" if wq[0] % 2 else "pool", lambda e, c0=c0, n=n: e.dma_start(out=stage[:, 0:n], in_=w_in[kc * 128:(kc + 1) * 128, c0:c0 + n]),
                 writes=[stage_buf], dma=True, nbytes=128 * n * 4)
            if perm:
                o = dst[:, kc, d0:d0 + n].rearrange("p (hp ab d) -> p hp ab d", hp=4, ab=2)
                i = stage[:, 0:n].rearrange("p (ab hp d) -> p hp ab d", hp=4, ab=2)
            else:
                o = dst[:, kc, d0:d0 + n]
                i = stage[:, 0:n]
            P.op("dve", lambda e, o=o, i=i: e.tensor_scalar(o, i, lnw_col[:, kc:kc + 1], None, ALU.mult),
                 reads=[stage_buf, B_small], writes=[dst_buf], n=n)

    def wk_set(A, N):
        return dict(xf=A.get((N,), F32), sq=A.get((N,), F32), ta=A.get((N // 2,), F32), tb=A.get((N // 2,), F32),
                    ssq=A.get((16,), F32), lnq=A.get((16,), F32), rsq=A.get((16,), F32),
                    buf=dict(xf=Buf(), sq=Buf(), ta=Buf(), tb=Buf(), st=Buf()))

    def run_job(job):
        nm, S, J, OWN = job["nm"], job["S"], job["J"], job["own"]
        r = 128 // J
        NT_OWN = OWN // 128
        K2 = NT_OWN
        N2 = 2 * K2 * r
        NCH = OWN // 512
        xf_d, xo_d, y_d = xfull[nm], xown[nm], yout[nm]
        x_str = xf_d.rearrange("(p j) c -> j p c", j=J)
        x_nat = xo_d.rearrange("(t p) c -> t p c", p=128)
        y_nat = y_d.rearrange("(t p) c -> t p c", p=128)
        P.strm = 0

        AJ = Alloc(PERSIST_END)
        kT = AJ.get((S,), BF16)
        vx = AJ.get((J, 192), BF16)
        foT = AJ.get((4, OWN), BF16)
        t1 = AJ.get((256,), F32)
        t2 = AJ.get((256,), F32)
        ga = AJ.get((N2,), BF16)
        gb = AJ.get((N2,), BF16)
        B_kT = [Buf() for _ in range(J)]
        B_vx = [Buf() for _ in range(J)]
        B_fo = [Buf() for _ in range(NCH)]
        B_tw, B_g = Buf(), Buf()
        JOB_END = AJ.off

        P.op("pool", lambda e: e.memset(vx, 1.0), writes=B_vx, n=J * 192)
        P.op("pool", lambda e: e.dma_start(out=t1, in_=t1_d[nm]), writes=[B_tw], dma=True, nbytes=131072)
        P.op("pool", lambda e: e.dma_start(out=t2, in_=t2_d[nm]), writes=[B_tw], dma=True, nbytes=131072)
        P.op("pool", lambda e: e.dma_start(out=ga, in_=ga_d[nm]), writes=[B_g], dma=True, nbytes=65536)
        P.op("pool", lambda e: e.dma_start(out=gb, in_=gb_d[nm]), writes=[B_g], dma=True, nbytes=65536)

        AF_ = Alloc(JOB_END)
        U = AF_.get((J, 512), BF16)
        xg = [AF_.get((K2, 2, 128), BF16) for _ in range(2)]
        F2_BASE = AF_.off
        w1 = AF_.get((8, 768), BF16)
        NXT = 3
        xt = [AF_.get((1024,), F32) for _ in range(NXT)]
        hb = [AF_.get((1024,), BF16) for _ in range(2)]
        hT = [AF_.get((8, 128), BF16) for _ in range(2)]
        junk = AF_.get((1024,), BF16)
        st_s = [AF_.get((4,), F32) for _ in range(2)]
        rt = [AF_.get((2, 64), F32) for _ in range(2)]
        wkk = [wk_set(AF_, 256) for _ in range(2)]
        kb = [AF_.get((256,), BF16) for _ in range(2)]
        _as = Alloc(JOB_END + J * 512 * 2)
        NSTG = 8 if nm == "s" else 8
        stage = [_as.get((512,), F32) for _ in range(NSTG)]
        AF2 = Alloc(F2_BASE)
        NS1 = 3
        p12 = [AF2.get((2, 2, 256), BF16) for _ in range(NS1)]
        y2 = [AF2.get((2, 2, 128), BF16) for _ in range(NS1)]
        xT3 = [AF2.get((2, 512), BF16) for _ in range(2)]
        gn = AF2.get((N2,), BF16)
        y1s = [AF2.get((2, 256), BF16) for _ in range(NS1)]
        t1b = AF2.get((256,), BF16)
        t2b = AF2.get((256,), BF16)
        B_y1s = [Buf() for _ in range(NS1)]
        B_p2 = [Buf() for _ in range(NS1)]
        B_p2b = [Buf() for _ in range(NS1)]
        B_twb = Buf()
        B_w1, B_U = Buf(), Buf()
        B_xg = [Buf(), Buf()]
        B_xt = [Buf() for _ in range(NXT)]
        B_hb, B_hT, B_st, B_rt, B_kb = ([Buf(), Buf()] for _ in range(5))
        B_junk = Buf()
        B_p12 = [Buf() for _ in range(NS1)]
        B_stage = [Buf() for _ in range(NSTG)]
        B_y2 = [Buf() for _ in range(NS1)]
        B_xT3 = [Buf(), Buf()]

        for kc in range(8):
            load_convert_win(w1, B_w1, kc, [(512, 256, 0, False)], stage[(2 * kc) % NSTG], B_stage[(2 * kc) % NSTG])
            load_convert_win(w1, B_w1, kc, [(1280, 512, 256, False)], stage[(2 * kc + 1) % NSTG], B_stage[(2 * kc + 1) % NSTG])

        chk(nm + "_w1", [("w1", w1.rearrange("p a b -> p (a b)"), [128, 8 * 768], BF16)])
        for jp in range(J // 2):
            bkv = 2 + (jp % 2)
            rts = jp % 2
            P.op("pool", lambda e, jp=jp, rts=rts: e.dma_start(out=rt[rts], in_=rk[nm][2 * jp:2 * jp + 2].rearrange("t p c -> p t c")),
                 writes=[B_rt[rts]], dma=True, nbytes=65536)
            for tl in range(2):
                j = 2 * jp + tl
                xs_, hs = j % NXT, j % 2
                P.op("sp", lambda e, j=j, xs_=xs_: e.dma_start(out=xt[xs_], in_=x_str[j]), writes=[B_xt[xs_]], dma=True, nbytes=524288)
                ss, lnv, rstd = st_s[hs][:, 0:1], st_s[hs][:, 1:2], st_s[hs][:, 2:3]
                rms_rstd(xt[xs_], B_xt[xs_], junk, B_junk, ss, lnv, rstd, B_st[hs], D)
                P.op("dve", lambda e, xs_=xs_, hs=hs, rstd=rstd: e.tensor_scalar(hb[hs], xt[xs_], rstd, None, ALU.mult),
                     reads=[B_xt[xs_], B_st[hs]], writes=[B_hb[hs]], n=600)
                bt = j % 2
                psT = bank(bt).bitcast(BF16)
                for kc in range(8):
                    P.op("pe", lambda e, kc=kc, hs=hs, psT=psT: e.transpose(psT[:, kc * 128:(kc + 1) * 128], hb[hs][:, kc * 128:(kc + 1) * 128], ident),
                         reads=[B_hb[hs], B_ident], writes=[PB[bt]], n=128)
                P.op("dve", lambda e, hs=hs, psT=psT: e.tensor_copy(hT[hs].rearrange("p a b -> p (a b)"), psT),
                     reads=[PB[bt]], writes=[B_hT[hs]], n=700)
                bu = 4 + (j % 3)
                for kc in range(8):
                    P.op("pe", lambda e, kc=kc, hs=hs, tl=tl, bkv=bkv: e.matmul(bank(bkv)[:, tl * 256:(tl + 1) * 256], hT[hs][:, kc, :], w1[:, kc, 0:256],
                                                                       start=(kc == 0), stop=(kc == 7)),
                         reads=[B_hT[hs], B_w1], writes=[PB[bkv]], n=256)
                    P.op("pe", lambda e, kc=kc, hs=hs, bu=bu: e.matmul(bank(bu), hT[hs][:, kc, :], w1[:, kc, 256:768], start=(kc == 0), stop=(kc == 7)),
                         reads=[B_hT[hs], B_w1], writes=[PB[bu]], n=512)
                P.op("act", lambda e, j=j, bu=bu: e.activation(U[:, j, :], bank(bu), AF.Copy), reads=[PB[bu]], writes=[B_U], n=512)
                P.op("act", lambda e, j=j, tl=tl, bkv=bkv: e.activation(
                    vx[:, j, :].rearrange("p (a b) -> p a b", b=64)[:, 0:3:2, :],
                    bank(bkv)[:, tl * 256 + 128:(tl + 1) * 256].rearrange("p (a b) -> p a b", b=64), AF.Copy),
                    reads=[PB[bkv]], writes=[B_vx[j]], n=128)
            ksrc = bank(bkv).rearrange("p (t c) -> p t c", t=2)[:, :, 0:128]
            kslot = jp % 2
            normrope(ksrc, [PB[bkv]], 2, 2, gk, rt[rts], B_rt[rts], kb[kslot], B_kb[kslot], wkk[kslot])
            psK = bank(7).bitcast(BF16)
            for tl in range(2):
                P.op("pe", lambda e, tl=tl, kslot=kslot, psK=psK: e.transpose(psK[:, tl * 128:(tl + 1) * 128], kb[kslot][:, tl * 128:(tl + 1) * 128], ident),
                     reads=[B_kb[kslot], B_ident], writes=[PB[7]], n=128)
            P.op("dve", lambda e, jp=jp, psK=psK: e.tensor_copy(kT[:, jp * 256:(jp + 1) * 256], psK[:, 0:256]),
                 reads=[PB[7]], writes=[B_kT[2 * jp], B_kT[2 * jp + 1]], n=256)

        chk(nm + "_pass1", [("kT", kT, [128, S], BF16), ("vx", vx.rearrange("p a b -> p (a b)"), [128, J * 192], BF16),
                            ("U", U.rearrange("p a b -> p (a b)"), [128, 512 * J], BF16)])
        P.barrier()
        P.op("pool", lambda e: e.dma_start(out=gn, in_=gn_d[nm]), writes=[B_g], dma=True, nbytes=65536)
        P.op("dve", lambda e: e.tensor_copy(t1b, t1), reads=[B_tw], writes=[B_twb], n=256)
        P.op("dve", lambda e: e.tensor_copy(t2b, t2), reads=[B_tw], writes=[B_twb], n=256)
        NB = 128 // r
        PER2 = 512 // N2
        for g in range(4):
            xs = g % 2
            for bp in range(NB // 2):
                s1 = (g * (NB // 2) + bp) % NS1
                b1 = 0 + s1
                for bl in range(2):
                    cb = bp * 2 + bl
                    for cl in range(r):
                        cidx = g * 128 + cb * r + cl
                        P.op("pe", lambda e, bl=bl, cl=cl, cidx=cidx, b1=b1: e.matmul(bank(b1)[cl * J:(cl + 1) * J, bl * 256:(bl + 1) * 256],
                                                                             U.rearrange("p j c -> p (j c)")[:, cidx:J * 512:512], f1, start=True, stop=True,
                                                                             tile_position=(0, cl * J)),
                             reads=[B_U, B_f1], writes=[PB[b1]], n=256 // r + 40)
                pv = bank(b1).rearrange("p (b c) -> p b c", b=2)
                P.op("act", lambda e, s1=s1, pv=pv: e.activation(y1s[s1], pv, AF.Copy), reads=[PB[b1]], writes=[B_y1s[s1]], n=512)
                P.op("dve", lambda e, s1=s1: e.tensor_tensor(p12[s1][:, 0, :, :], y1s[s1], t1b.unsqueeze(1).to_broadcast([128, 2, 256]), ALU.mult),
                     reads=[B_y1s[s1], B_twb], writes=[B_p12[s1]], n=300)
                P.op("pool", lambda e, s1=s1: e.tensor_tensor(p12[s1][:, 1, 1, :], y1s[s1][:, 1, :], t2b, ALU.mult),
                     reads=[B_y1s[s1], B_twb], writes=[B_p2[s1]], n=256)
                P.op("dve", lambda e, s1=s1: e.tensor_tensor(p12[s1][:, 1, 0, :], y1s[s1][:, 0, :], t2b, ALU.mult),
                     reads=[B_y1s[s1], B_twb], writes=[B_p2b[s1]], n=200)
                for bl in range(2):
                    cb = bp * 2 + bl
                    grp2 = cb // PER2
                    pos2 = cb % PER2
                    b2 = 3 + (grp2 % 2)
                    terms = ((0, 0, ga), (1, 128, gn), (1, 0, gb), (0, 128, gb))
                    for ti, (wh, c0, gm) in enumerate(terms):
                        P.op("pe", lambda e, bl=bl, s1=s1, b2=b2, pos2=pos2, wh=wh, c0=c0, gm=gm, ti=ti: e.matmul(
                            bank(b2)[:, pos2 * N2:(pos2 + 1) * N2], p12[s1][:, wh, bl, c0:c0 + 128], gm, start=(ti == 0), stop=(ti == 3)),
                             reads=[B_p12[s1], B_p2[s1], B_p2b[s1], B_g], writes=[PB[b2]], n=N2)
                    if pos2 == PER2 - 1:
                        cb0 = grp2 * PER2
                        for ri in range(2):
                            src = bank(b2).rearrange("p (b ri k c) -> p b ri k c", b=PER2, ri=2, k=K2)[:, :, ri, :, :]
                            dst = xg[xs][:, :, ri, cb0 * r:(cb0 + PER2) * r].rearrange("p k (b c) -> p b k c", b=PER2)
                            P.op("act", lambda e, src=src, dst=dst: e.activation(dst, src, AF.Copy), reads=[PB[b2]], writes=[B_xg[xs]], n=256)
            for ch in range(NCH):
                s3 = ch % 2
                b3 = 5
                psT3 = bank(b3).bitcast(BF16).rearrange("p (ri t k) -> p ri t k", ri=2, t=4)
                for t in range(4):
                    for ri in range(2):
                        P.op("pe", lambda e, t=t, ri=ri, ch=ch, xs=xs, psT3=psT3: e.transpose(psT3[:, ri, t, :], xg[xs][:, ch * 4 + t, ri, :], ident),
                             reads=[B_xg[xs], B_ident], writes=[PB[b3]], n=128)
                P.op("act", lambda e, s3=s3, b3=b3: e.activation(xT3[s3].rearrange("p a b -> p (a b)"), bank(b3).bitcast(BF16), AF.Copy),
                     reads=[PB[b3]], writes=[B_xT3[s3]], n=1024)
                b4 = 6 + s3
                for ri in range(2):
                    P.op("pe", lambda e, ri=ri, s3=s3, b4=b4, g=g: e.matmul(bank(b4), qc[:, g, ri, :], xT3[s3][:, ri, :], start=(ri == 0), stop=(ri == 1)),
                         reads=[B_qc, B_xT3[s3]], writes=[PB[b4]], n=512)
                P.op("act", lambda e, b4=b4, g=g, ch=ch: e.activation(foT[:, g, ch * 512:(ch + 1) * 512], bank(b4), AF.Identity, bias=bfc[:, g:g + 1]),
                     reads=[PB[b4], B_small], writes=[B_fo[ch]], n=512)
        chk(nm + "_four", [("foT", foT.rearrange("p a b -> p (a b)"), [128, 4 * OWN], BF16),
                           ("xg1", xg[1].rearrange("p a b c -> p (a b c)"), [128, K2 * 256], BF16)])
        P.barrier()

        A2 = Alloc(JOB_END)
        w2 = A2.get((8, 1536), BF16)
        wo = A2.get((8, 1024), BF16)
        xt2 = [A2.get((1024,), F32) for _ in range(2)]
        hb2 = [A2.get((1024,), BF16) for _ in range(2)]
        hTc = A2.get((8, 512), BF16)
        junk2 = A2.get((1024,), BF16)
        st2 = [A2.get((4,), F32) for _ in range(4)]
        rt2 = [A2.get((2, 64), F32) for _ in range(2)]
        TP = 2 if nm == "p" else 1
        wkq = [wk_set(A2, 512 * TP)]
        qb = [A2.get((512 * TP,), BF16) for _ in range(2)]
        qT = [A2.get((4, 512), BF16) for _ in range(2)]
        sga = [A2.get((4, 512), BF16) for _ in range(2)]
        sgf = [A2.get((4, 512), BF16) for _ in range(2)]
        NPT = 4 if nm == "s" else 3
        pT = [A2.get((1024,), BF16) for _ in range(NPT)]
        _yoff = A2.off
        yT = [A2.get((8, 512), BF16) for _ in range(2)]
        oab = A2.get((2, 512), F32)
        NEG = 1 if nm == "p" else 2
        eg = [A2.get((512,), F32) for _ in range(NEG)]
        B_eg = [Buf() for _ in range(NEG)]
        rc = A2.get((512,), F32)
        rr = [A2.get((1024,), F32) for _ in range(2)]
        _ay = Alloc(_yoff)
        stage2 = rr + [_ay.get((1024,), F32) for _ in range(4)]
        B_w2, B_wo, B_hTc, B_junk2, B_oab, B_rc = (Buf() for _ in range(6))
        B_xt2, B_hb2, B_rt2, B_qb, B_rr, B_qT, B_sga, B_sgf = ([Buf(), Buf()] for _ in range(8))
        B_st2 = [Buf() for _ in range(4)]
        B_stage2 = B_rr + [Buf() for _ in range(4)]
        B_pT = [Buf() for _ in range(NPT)]
        B_yT = [[Buf() for _ in range(8)] for _ in range(2)]

        nsl = [0]

        def nslot():
            nsl[0] += 1
            return nsl[0] % 6
        for c8 in range(8):
            sl = nslot()
            wq[0] += 1
            qn = "sp" if wq[0] % 2 else "pool"
            if c8 < 4:
                P.op(qn, lambda e, c8=c8, sl=sl: e.dma_start(out=stage2[sl][0:64, :], in_=w_out[64 * c8:64 * c8 + 64, :]),
                     writes=[B_stage2[sl]], dma=True, nbytes=262144)
                P.op(qn, lambda e, c8=c8, sl=sl: e.dma_start(out=stage2[sl][64:128, :], in_=w_out[256 + 64 * c8:256 + 64 * c8 + 64, :]),
                     writes=[B_stage2[sl]], dma=True, nbytes=262144)
            else:
                P.op(qn, lambda e, c8=c8, sl=sl: e.dma_start(out=stage2[sl], in_=w_out[512 + 128 * (c8 - 4):512 + 128 * (c8 - 3), :]),
                     writes=[B_stage2[sl]], dma=True, nbytes=524288)
            P.op("dve", lambda e, c8=c8, sl=sl: e.tensor_copy(wo[:, c8, :], stage2[sl]), reads=[B_stage2[sl]], writes=[B_wo], n=1024)
        for kc in range(8):
            for sec in [(0, 512, 0, True), (768, 512, 512, True), (1792, 512, 1024, False)]:
                sl = nslot()
                load_convert_win(w2, B_w2, kc, [sec], stage2[sl], B_stage2[sl])

        chk(nm + "_w2", [("w2", w2.rearrange("p a b -> p (a b)"), [128, 8 * 1536], BF16), ("wo", wo.rearrange("p a b -> p (a b)"), [128, 8 * 1024], BF16)])
        NKT = J
        cnt = dict(x=0, st=0, bg=0, step=0)

        def part_a(ch):
            P.strm = 1
            cs = ch % 2
            for tp in range(4 // TP):
                qs = (ch * (4 // TP) + tp) % 2
                tg0 = ch * 4 + tp * TP
                P.op("pool", lambda e, tg0=tg0, qs=qs: e.dma_start(out=rt2[qs][:, 0:TP, :], in_=rq[nm][tg0:tg0 + TP].rearrange("t p c -> p t c")),
                     writes=[B_rt2[qs]], dma=True, nbytes=32768 * TP)
                banks = []
                for tl in range(TP):
                    t = tp * TP + tl
                    tg = ch * 4 + t
                    xs_, hs = cnt["x"] % 2, tg % 2
                    cnt["x"] += 1
                    sts = cnt["st"] % 4
                    cnt["st"] += 1
                    P.op("sp", lambda e, tg=tg, xs_=xs_: e.dma_start(out=xt2[xs_], in_=x_nat[tg]), writes=[B_xt2[xs_]], dma=True, nbytes=524288)
                    ss, lnv, rstd = st2[sts][:, 0:1], st2[sts][:, 1:2], st2[sts][:, 2:3]
                    rms_rstd(xt2[xs_], B_xt2[xs_], junk2, B_junk2, ss, lnv, rstd, B_st2[sts], D, sq_eng="dve")
                    P.op("dve", lambda e, xs_=xs_, hs=hs, rstd=rstd: e.tensor_scalar(hb2[hs], xt2[xs_], rstd, None, ALU.mult),
                         reads=[B_xt2[xs_], B_st2[sts]], writes=[B_hb2[hs]], n=600)
                    if TP == 1:
                        ba, bb = 6 + (t % 2), 7 - (t % 2)
                    else:
                        ba = bb = 6 + tl
                    banks.append(bb)
                    psT = bank(ba).bitcast(BF16)
                    for kc in range(8):
                        P.op("pe", lambda e, kc=kc, hs=hs, psT=psT: e.transpose(psT[:, kc * 128:(kc + 1) * 128], hb2[hs][:, kc * 128:(kc + 1) * 128], ident),
                             reads=[B_hb2[hs], B_ident], writes=[PB[ba]], n=128)
                    P.op("dve", lambda e, t=t, psT=psT: e.tensor_copy(hTc[:, :, t * 128:(t + 1) * 128], psT.rearrange("p (a b) -> p a b", a=8)),
                         reads=[PB[ba]], writes=[B_hTc], n=700)
                    for kc in range(8):
                        P.op("pe", lambda e, kc=kc, t=t, bb=bb: e.matmul(bank(bb), hTc[:, kc, t * 128:(t + 1) * 128], w2[:, kc, 0:512], start=(kc == 0), stop=(kc == 7)),
                             reads=[B_hTc, B_w2], writes=[PB[bb]], n=512)
                if TP == 1:
                    qsrc, qsb = bank(banks[0]).rearrange("p (t c) -> p t c", t=1), [PB[banks[0]]]
                else:
                    qsrc, qsb = bank(6, 2).rearrange("p (t c) -> p t c", t=2), [PB[6], PB[7]]
                normrope(qsrc, qsb, TP, 8, gq, rt2[qs][:, 0:TP, :], B_rt2[qs], qb[qs], B_qb[qs], wkq[0], evac=True)
                bq = 6 + (tp % 2) if TP == 1 else 6
                psQ = bank(bq).bitcast(BF16)
                for tl in range(TP):
                    for hp in range(4):
                        P.op("pe", lambda e, hp=hp, tl=tl, qs=qs, psQ=psQ: e.transpose(psQ[:, (tl * 4 + hp) * 128:(tl * 4 + hp + 1) * 128],
                                                                                    qb[qs][:, (tl * 4 + hp) * 128:(tl * 4 + hp + 1) * 128], ident),
                             reads=[B_qb[qs], B_ident], writes=[PB[bq]], n=128)
                t0 = tp * TP
                P.op("dve", lambda e, t0=t0, psQ=psQ, cs=cs: e.tensor_copy(
                    qT[cs][:, :, t0 * 128:(t0 + TP) * 128].rearrange("p h (tl k) -> p tl h k", tl=TP),
                    psQ[:, 0:512 * TP].rearrange("p (tl h k) -> p tl h k", tl=TP, h=4)),
                    reads=[PB[bq]], writes=[B_qT[cs]], n=400 * TP)
            if ch == 0:
                chk(nm + "_t0", [("qT", qT[0].rearrange("p a b -> p (a b)"), [128, 2048], BF16), ("hTc", hTc.rearrange("p a b -> p (a b)"), [128, 4096], BF16)])
            for fc in range(8):
                bg = 6 + (cnt["bg"] % 2)
                cnt["bg"] += 1
                for kc in range(8):
                    P.op("pe", lambda e, kc=kc, fc=fc, bg=bg: e.matmul(bank(bg), w2[:, kc, 512 + fc * 128:512 + (fc + 1) * 128], hTc[:, kc, :],
                                                                   start=(kc == 0), stop=(kc == 7)),
                         reads=[B_hTc, B_w2], writes=[PB[bg]], n=512)
                es = cnt["bg"] % NEG
                dst = sga[cs][:, fc, :] if fc < 4 else sgf[cs][:, fc - 4, :]
                dbuf = B_sga[cs] if fc < 4 else B_sgf[cs]
                P.op("act", lambda e, bg=bg, es=es: e.activation(eg[es], bank(bg), AF.Exp, scale=-1.0), reads=[PB[bg]], writes=[B_eg[es]], n=512)
                P.op("dve", lambda e, es=es: e.tensor_scalar(eg[es], eg[es], 1.0, None, ALU.add), reads=[B_eg[es]], writes=[B_eg[es]], n=300)
                P.op("dve", lambda e, es=es: e.reciprocal(eg[es], eg[es]), reads=[B_eg[es]], writes=[B_eg[es]], n=3000)
                P.op("dve", lambda e, bg=bg, es=es, dst=dst: e.tensor_tensor(dst, bank(bg), eg[es], ALU.mult), reads=[PB[bg], B_eg[es]], writes=[dbuf], n=512)
                if fc >= 4:
                    P.op("pool", lambda e, fc=fc, ch=ch, cs=cs: e.tensor_tensor(yT[cs][:, fc, :], foT[:, fc - 4, ch * 512:(ch + 1) * 512], sgf[cs][:, fc - 4, :], ALU.mult),
                         reads=[B_fo[ch], B_sgf[cs], B_w2, B_wo], writes=[B_yT[cs][fc]], n=512)
            if ch == 0:
                chk(nm + "_A0", [("qT", qT[0].rearrange("p a b -> p (a b)"), [128, 2048], BF16), ("sga", sga[0].rearrange("p a b -> p (a b)"), [128, 2048], BF16),
                                 ("sgf", sgf[0].rearrange("p a b -> p (a b)"), [128, 2048], BF16), ("yT", yT[0].rearrange("p a b -> p (a b)"), [128, 4096], BF16)])

        def part_b(ch):
            P.strm = 0
            cs = ch % 2
            steps = [(hp, kt) for hp in range(4) for kt in range(NKT)]

            def qk(i):
                hp, kt = steps[i]
                sb = 2 * (i % 2)
                P.op("pe", lambda e, kt=kt, sb=sb, hp=hp: e.matmul(bank(sb), kT[0:64, kt * 128:(kt + 1) * 128], qT[cs][0:64, hp, :], start=True, stop=True),
                     reads=[B_kT[kt], B_qT[cs]], writes=[PB[sb]], n=300)
                P.op("pe", lambda e, kt=kt, sb=sb, hp=hp: e.matmul(bank(sb + 1), kT[64:128, kt * 128:(kt + 1) * 128], qT[cs][64:128, hp, :], start=True, stop=True),
                     reads=[B_kT[kt], B_qT[cs]], writes=[PB[sb + 1]], n=300)

            def ex(i):
                sb = 2 * (i % 2)
                ps = i % NPT
                P.op("act", lambda e, sb=sb, ps=ps: e.activation(pT[ps], bank(sb, 2), AF.Exp, scale=0.125),
                     reads=[PB[sb], PB[sb + 1]], writes=[B_pT[ps]], n=1024)

            def pv(i):
                hp, kt = steps[i]
                ps = i % NPT
                P.op("pe", lambda e, kt=kt, ps=ps: e.matmul(bank(4), vx[:, kt, 0:128], pT[ps][:, 0:512], start=(kt == 0), stop=(kt == NKT - 1)),
                     reads=[B_vx[kt], B_pT[ps]], writes=[PB[4]], n=512)
                P.op("pe", lambda e, kt=kt, ps=ps: e.matmul(bank(5), vx[:, kt, 64:192], pT[ps][:, 512:1024], start=(kt == 0), stop=(kt == NKT - 1)),
                     reads=[B_vx[kt], B_pT[ps]], writes=[PB[5]], n=512)

            def finalize(hp):
                P.op("dve", lambda e: e.tensor_copy(oab[:, 0, :], bank(4)), reads=[PB[4]], writes=[B_oab], n=512)
                P.op("dve", lambda e: e.tensor_copy(oab[:, 1, :], bank(5)), reads=[PB[5]], writes=[B_oab], n=512)
                P.op("dve", lambda e: e.tensor_copy(rc[0:64, :], oab[64:128, 0, :]), reads=[B_oab], writes=[B_rc], n=256)
                P.op("dve", lambda e: e.tensor_copy(rc[64:128, :], oab[0:64, 1, :]), reads=[B_oab], writes=[B_rc], n=256)
                P.op("dve", lambda e: e.reciprocal(rc, rc), reads=[B_rc], writes=[B_rc], n=3000)
                P.op("pool", lambda e, hp=hp: e.tensor_tensor(rc, rc, sga[cs][:, hp, :], ALU.mult), reads=[B_rc, B_sga[cs]], writes=[B_rc], n=512)
                P.op("dve", lambda e, hp=hp: e.tensor_tensor(yT[cs][0:64, hp, :], oab[0:64, 0, :], rc[0:64, :], ALU.mult),
                     reads=[B_oab, B_rc, B_w2, B_wo], writes=[B_yT[cs][hp]], n=512)
                P.op("dve", lambda e, hp=hp: e.tensor_tensor(yT[cs][64:128, hp, :], oab[64:128, 1, :], rc[64:128, :], ALU.mult),
                     reads=[B_oab, B_rc, B_w2, B_wo], writes=[B_yT[cs][hp]], n=512)

            NS = len(steps)
            qk(0)
            qk(1)
            for i in range(NS):
                ex(i)
                if i + 2 < NS:
                    qk(i + 2)
                pv(i)
                if steps[i][1] == NKT - 1:
                    finalize(steps[i][0])
            if ch == 0:
                chk(nm + "_B0", [("yT", yT[0].rearrange("p a b -> p (a b)"), [128, 4096], BF16)])

        def part_c(ch):
            P.strm = 1
            cs = ch % 2
            for t in range(4):
                tg = ch * 4 + t
                xs_, os_ = cnt["x"] % 2, tg % 2
                cnt["x"] += 1
                sts = cnt["st"] % 4
                cnt["st"] += 1
                P.op("sp", lambda e, tg=tg, xs_=xs_: e.dma_start(out=xt2[xs_], in_=x_nat[tg]), writes=[B_xt2[xs_]], dma=True, nbytes=524288)
                for half in range(2):
                    for c8 in range(8):
                        P.op("pe", lambda e, c8=c8, half=half, t=t: e.matmul(bank(6 + half), yT[cs][:, c8, t * 128:(t + 1) * 128], wo[:, c8, half * 512:(half + 1) * 512],
                                                                     start=(c8 == 0), stop=(c8 == 7)),
                             reads=[B_yT[cs][c8], B_wo], writes=[PB[6 + half]], n=512)
                P.op("dve", lambda e, xs_=xs_, os_=os_: e.tensor_tensor(rr[os_], bank(6, 2), xt2[xs_], ALU.add),
                     reads=[PB[6], PB[7], B_xt2[xs_], B_w2, B_wo], writes=[B_rr[os_]], n=1024)
                ss, lnv, rstd = st2[sts][:, 0:1], st2[sts][:, 1:2], st2[sts][:, 2:3]
                rms_rstd(rr[os_], B_rr[os_], junk2, B_junk2, ss, lnv, rstd, B_st2[sts], D, sq_eng="dve")
                P.op("dve", lambda e, os_=os_, rstd=rstd: e.scalar_tensor_tensor(rr[os_], rr[os_], rstd, fn_bc, ALU.mult, ALU.mult),
                     reads=[B_rr[os_], B_st2[sts], B_fn], writes=[B_rr[os_]], n=1024)
                P.op("sp", lambda e, tg=tg, os_=os_: e.dma_start(out=y_nat[tg], in_=rr[os_]), reads=[B_rr[os_]], dma=True, nbytes=524288)

        P.group_begin(); part_a(0); P.group_end(-1.0, -0.01)
        for ch in range(NCH):
            if ch + 1 < NCH:
                P.group_begin(); part_a(ch + 1); P.group_end(ch + 0.0, ch + 0.60)
            P.group_begin(); part_b(ch); P.group_end(float(ch), ch + 1.0)
            if ch + 1 < NCH:
                P.group_begin(); part_c(ch); P.group_end(ch + 1.0, ch + 1.5)
            else:
                P.group_begin(); part_c(ch); P.group_end(float(NCH), NCH + 1.0)
        P.strm = 0
        P.barrier()

    try:
        for job in JOBS:
            if not stopped:
                run_job(job)
    except _Stop:
        pass

    P.schedule()
    P.emit()
    st.close()
    return nc


_NC = None
_CONST = None


def _consts():
    global _CONST
    if _CONST is not None:
        return _CONST
    c = {}
    p = np.arange(128)
    c["rk_s"] = np.stack([rope_tab(64 * p + j) for j in range(64)]).astype(np.float32)
    c["rk_p"] = np.stack([rope_tab(32 * p + j) for j in range(32)]).astype(np.float32)
    c["rq_p"] = np.stack([rope_tab(128 * k2 + p) for k2 in range(32)]).astype(np.float32)
    c["rq_s"] = [np.stack([rope_tab(2048 * qr + 128 * k2 + p) for k2 in range(16)]).astype(np.float32) for qr in range(4)]
    c["f1"] = f1_mat().astype(np.float32).astype(bf)
    c["t1_s"], c["t2_s"] = twiddle(64)
    c["t1_p"], c["t2_p"] = twiddle(32)
    gs = [g_mats(64, list(range(16 * qr, 16 * qr + 16))) for qr in range(4)]
    c["ga_s"] = [g[0].astype(np.float32).astype(bf) for g in gs]
    c["gb_s"] = [g[1].astype(np.float32).astype(bf) for g in gs]
    c["gn_s"] = [(-g[0]).astype(np.float32).astype(bf) for g in gs]
    gp = g_mats(32, list(range(32)))
    c["ga_p"] = gp[0].astype(np.float32).astype(bf)
    c["gb_p"] = gp[1].astype(np.float32).astype(bf)
    c["gn_p"] = (-gp[0]).astype(np.float32).astype(bf)
    c["cc"], c["sc"] = cs_mats()
    _CONST = c
    return c


def kernel(x_prompt, x_sample, ln_w, w_in, q_norm, k_norm, w_fourier, b_fourier, w_out, final_norm):
    global _NC
    if _NC is None:
        _NC = build()
    nc = _NC
    c = _consts()
    f32 = np.float32
    x_prompt = np.asarray(x_prompt, f32)
    x_sample = np.asarray(x_sample, f32)
    shared = {
        "ln_w": np.ascontiguousarray(np.asarray(ln_w, f32)[0]),
        "w_in": np.ascontiguousarray(np.asarray(w_in, f32)[0]),
        "q_norm": np.ascontiguousarray(np.asarray(q_norm, f32)[0]),
        "k_norm": np.ascontiguousarray(np.asarray(k_norm, f32)[0]),
        "w_f": np.ascontiguousarray(np.asarray(w_fourier, f32)[0]),
        "b_f": np.ascontiguousarray(np.asarray(b_fourier, f32)[0]),
        "w_out": np.ascontiguousarray(np.asarray(w_out, f32)[0]),
        "fnorm": np.ascontiguousarray(np.asarray(final_norm, f32)),
        "rk_s": c["rk_s"], "rk_p": c["rk_p"], "rq_p": c["rq_p"], "f1": c["f1"],
        "t1_s": c["t1_s"], "t2_s": c["t2_s"], "t1_p": c["t1_p"], "t2_p": c["t2_p"],
        "ga_p": c["ga_p"], "gb_p": c["gb_p"], "gn_p": c["gn_p"], "cc": c["cc"], "sc": c["sc"],
    }
    in_maps = []
    for core in range(N_CORES):
        b, qr = core // 4, core % 4
        m = dict(shared)
        m["xp"] = np.ascontiguousarray(x_prompt[core])
        m["xs"] = np.ascontiguousarray(x_sample[b])
        m["xo"] = np.ascontiguousarray(x_sample[b, 2048 * qr:2048 * (qr + 1)])
        m["rq_s"] = c["rq_s"][qr]
        m["ga_s"] = c["ga_s"][qr]
        m["gb_s"] = c["gb_s"][qr]
        m["gn_s"] = c["gn_s"][qr]
        in_maps.append(m)
    res = run_bass_kernel_spmd(nc, in_maps, core_ids=list(range(N_CORES)))
    y_prompt = np.stack([np.asarray(res.results[core]["yp"], f32) for core in range(N_CORES)])
    y_sample = np.empty((2, 8192, D), f32)
    for core in range(N_CORES):
        b, qr = core // 4, core % 4
        y_sample[b, 2048 * qr:2048 * (qr + 1)] = np.asarray(res.results[core]["yo"], f32)
    return (y_prompt, y_sample)
```

```python
import contextlib
import numpy as np
import ml_dtypes
import concourse.bass as bass
import concourse.mybir as mybir
from concourse.bass_utils import run_bass_kernel_spmd

F32 = mybir.dt.float32
BF16 = mybir.dt.bfloat16
AF = mybir.ActivationFunctionType
ALU = mybir.AluOpType
AX = mybir.AxisListType
bf = ml_dtypes.bfloat16

D = 1024
INW = 2304
EPS = 1e-6
N_CORES = 8


class Buf:
    __slots__ = ("name", "ws", "rs", "excl")

    def __init__(self, name="", excl=False):
        self.name = name
        self.ws = []
        self.rs = []
        self.excl = excl


class Op:
    __slots__ = ("eng", "fn", "deps", "ords", "sig", "cnt", "dma", "dsem", "dcnt", "c", "idx", "seg", "strm",
                 "t0", "t1", "nbytes", "done", "prio")

    def __init__(self, eng, fn, dma):
        self.eng = eng
        self.fn = fn
        self.deps = []
        self.ords = []
        self.sig = False
        self.cnt = 0
        self.dma = dma
        self.dsem = None
        self.dcnt = 0
        self.done = False


ENGS = ("pe", "act", "dve", "pool", "sp")


def _cost(eng, dma, n, nbytes):
    if dma:
        return 0.06
    if eng == "pe":
        return n / 2400.0 + 0.03
    if eng == "act":
        return (n + 150) / 1200.0
    if eng == "dve":
        return (n + 150) / 960.0
    return n * 0.002 + 0.2


class Prog:
    def __init__(self, nc, n_dma_sems=8):
        self.nc = nc
        self.ops = {e: [] for e in ENGS}
        self.n_dma_sems = n_dma_sems
        self.dma_rr = {e: 0 for e in ENGS}
        self.dma_last = {}
        self.seg = 0
        self.strm = 0
        self.nops = 0
        self.grp = None

    def op(self, eng, fn, reads=(), writes=(), dma=False, n=128, nbytes=0):
        o = Op(eng, fn, dma)
        o.c = _cost(eng, dma, n, nbytes)
        o.nbytes = nbytes
        o.idx = self.nops
        self.nops += 1
        o.seg = self.seg
        o.strm = self.strm
        o.prio = 0.0
        if self.grp is not None:
            self.grp.append(o)
        deps = o.deps
        excl_reads = [b for b in reads if b.excl]
        if excl_reads:
            reads = [b for b in reads if not b.excl]
            writes = list(writes) + [b for b in excl_reads if b not in writes]
        for b in reads:
            for w in b.ws:
                if w is not o:
                    deps.append(w)
            b.rs.append(o)
        for b in writes:
            for w in b.ws:
                if w is o:
                    continue
                if eng == "pe" and w.eng == "pe" and not w.dma and not dma:
                    o.ords.append(w)
                else:
                    deps.append(w)
            for r in b.rs:
                if r is not o:
                    deps.append(r)
            b.ws = [o]
            b.rs = []
        if dma:
            slot = self.dma_rr[eng]
            self.dma_rr[eng] = (slot + 1) % self.n_dma_sems
            prev = self.dma_last.get((eng, slot))
            if prev is not None:
                deps.append(prev)
            self.dma_last[(eng, slot)] = o
            o.dsem = (eng, slot)
        self.ops[eng].append(o)
        return o

    def group_begin(self):
        self.grp = []

    def group_end(self, p0, p1):
        g, self.grp = self.grp, None
        n = max(len(g), 1)
        for i, o in enumerate(g):
            o.prio = p0 + (p1 - p0) * i / n

    def barrier(self):
        self.seg += 1

    def schedule(self, W=96):
        HOP = 0.5
        BG_SLACK = 2.0
        self.seg_span = []
        allops = [o for e in ENGS for o in self.ops[e]]
        nseg = self.seg + 1
        new = {e: [] for e in ENGS}
        for sg in range(nseg):
            qs = {}
            for e in ENGS:
                for o in self.ops[e]:
                    if o.seg == sg:
                        qs.setdefault((e, o.strm), []).append(o)
            heads = {k: 0 for k in qs}
            free = {e: 0.0 for e in ENGS}
            dma_free = 0.0
            remaining = sum(len(v) for v in qs.values())
            out = {e: [] for e in ENGS}
            while remaining:
                best = None
                for (e, sid), lst in qs.items():
                    h = heads[(e, sid)]
                    while h < len(lst) and lst[h].done:
                        h += 1
                    heads[(e, sid)] = h
                    cnt = 0
                    i = h
                    fe = free[e]
                    while i < len(lst) and cnt < W:
                        o = lst[i]
                        i += 1
                        if o.done:
                            continue
                        cnt += 1
                        rdy = 0.0
                        ok = True
                        for d in o.deps:
                            if d.seg != sg:
                                continue
                            if not d.done:
                                ok = False
                                break
                            t = d.t1 + (HOP if d.eng != e or d.dma else 0.02)
                            if t > rdy:
                                rdy = t
                        if ok:
                            for d in o.ords:
                                if d.seg == sg and not d.done:
                                    ok = False
                                    break
                        if not ok:
                            continue
                        if sid == 1 and e == "pe":
                            rdy += BG_SLACK
                        stt = rdy if rdy > fe else fe
                        key = (stt, o.prio, o.idx)
                        if best is None or key < best[0]:
                            best = (key, o)
                        if stt <= fe:
                            break
                assert best is not None, "scheduler stuck (dependency cycle?)"
                (stt, _, _), o = best
                o.t0 = stt
                if o.dma:
                    xs = max(stt, dma_free)
                    dma_free = xs + o.nbytes / 150000.0
                    o.t1 = dma_free + 2.0
                    free[o.eng] = stt + o.c
                else:
                    o.t1 = stt + o.c
                    free[o.eng] = o.t1
                o.done = True
                out[o.eng].append(o)
                remaining -= 1
            for e in ENGS:
                new[e].extend(out[e])
            self.seg_span.append(max([o.t1 for e in ENGS for o in out[e]] + [0.0]))
        self.ops = new

    def emit(self):
        nc = self.nc
        nseg = self.seg + 1
        lastc = [dict() for _ in range(nseg)]
        lastd = [dict() for _ in range(nseg)]
        for e in ENGS:
            for o in self.ops[e]:
                if o.dma:
                    lastd[o.seg][o.dsem] = o
                else:
                    lastc[o.seg][e] = o
        for e in ENGS:
            cur = -1
            for o in self.ops[e]:
                if o.seg != cur:
                    for sg in range(max(cur, 0), o.seg):
                        o.deps.extend(lastc[sg].values())
                        o.deps.extend(lastd[sg].values())
                    cur = o.seg
        for e in ENGS:
            for o in self.ops[e]:
                for d in o.deps:
                    d.sig = True
        with contextlib.ExitStack() as st:
            esem = {e: st.enter_context(nc.semaphore("c_" + e)) for e in ENGS}
            dsem = {}
            for e in ENGS:
                if any(o.dma for o in self.ops[e]):
                    for s in range(self.n_dma_sems):
                        dsem[(e, s)] = st.enter_context(nc.semaphore("d_%s%d" % (e, s)))
            dcount = {k: 0 for k in dsem}
            for e in ENGS:
                c = 0
                for o in self.ops[e]:
                    if o.dma:
                        dcount[o.dsem] += 16
                        o.dcnt = dcount[o.dsem]
                    elif o.sig:
                        c += 1
                        o.cnt = c
            block = st.enter_context(nc.Block())
            engobj = {"pe": block.tensor, "act": block.scalar, "dve": block.vector,
                      "pool": block.gpsimd, "sp": block.sync}

            def make(e):
                ops = self.ops[e]

                def body(eng):
                    seen = {}
                    for o in ops:
                        need = {}
                        for d in o.deps:
                            if d.dma:
                                k = ("d",) + d.dsem
                                v = d.dcnt
                                sem = dsem[d.dsem]
                            else:
                                k = ("c", d.eng)
                                v = d.cnt
                                sem = esem[d.eng]
                            if seen.get(k, 0) >= v:
                                continue
                            if k not in need or need[k][1] < v:
                                need[k] = (sem, v)
                        for k, (sem, v) in need.items():
                            eng.wait_ge(sem, v)
                            seen[k] = v
                        ins = o.fn(eng)
                        if o.dma:
                            ins.then_inc(dsem[o.dsem], 16)
                        elif o.sig:
                            ins.then_inc(esem[e], 1)
                    if e == "sp":
                        for k, sem in dsem.items():
                            if dcount[k] > 0:
                                eng.wait_ge(sem, dcount[k])
                return body

            for e in ENGS:
                if self.ops[e] or e == "sp":
                    engobj[e](make(e))


def rope_tab(tok):
    tok = np.asarray(tok)
    row = (tok // 64).astype(np.float32)
    col = (tok % 64).astype(np.float32)
    inv = (np.float32(10000.0) ** (-np.arange(0, 32, 2, dtype=np.float32) / np.float32(32))).astype(np.float32)
    ang = np.concatenate([row[..., None] * inv, col[..., None] * inv], axis=-1).astype(np.float32)
    return np.concatenate([np.cos(ang), np.sin(ang)], axis=-1).astype(np.float32)


def f1_mat():
    p = np.arange(128)[:, None]
    k = np.arange(128)[None, :]
    a = 2 * np.pi * p * k / 128.0
    return np.concatenate([np.cos(a), -np.sin(a)], axis=1) / np.sqrt(128.0)


def twiddle(J):
    S = 128 * J
    j = (np.arange(128) % J)[:, None]
    k1 = np.arange(128)[None, :]
    a = 2 * np.pi * j * k1 / S
    tr, ti = np.cos(a), -np.sin(a)
    return np.concatenate([tr, tr], 1).astype(np.float32), np.concatenate([ti, ti], 1).astype(np.float32)


def g_mats(J, k2list):
    r = 128 // J
    K2 = len(k2list)
    m = np.arange(128)
    cl = m // J
    j = m % J
    n1 = K2 * r
    Gr = np.zeros((128, n1))
    Gi = np.zeros((128, n1))
    for k2i, k2 in enumerate(k2list):
        for c in range(r):
            n = k2i * r + c
            sel = cl == c
            a = 2 * np.pi * j[sel] * k2 / J
            Gr[sel, n] = np.cos(a) / np.sqrt(J)
            Gi[sel, n] = -np.sin(a) / np.sqrt(J)
    return np.concatenate([Gr, Gi], 1), np.concatenate([-Gi, Gr], 1)


def cs_mats():
    c = np.arange(128)
    a = 2 * np.pi * np.outer(c, c) / 128.0
    return (np.cos(a) / np.sqrt(128.0)).astype(np.float32), (np.sin(a) / np.sqrt(128.0)).astype(np.float32)


JOBS = (
    dict(nm="s", S=8192, J=64, own=2048),
    dict(nm="p", S=4096, J=32, own=4096),
)


class _Stop(Exception):
    pass


def build(stop=None):
    nc = bass.Bass("TRN2", target_bir_lowering=False)
    dbg_out = {}

    def chk(name, dumps):
        if stop != name:
            return
        P.barrier()
        for i, (dn, ap, shape, dt) in enumerate(dumps):
            d = nc.dram_tensor("dbg_" + dn, list(shape), dt, kind="ExternalOutput").ap()
            dbg_out[dn] = d
            P.op("sp", lambda e, d=d, ap=ap: e.dma_start(out=d, in_=ap), dma=True)
        raise _Stop()

    def din(name, shape, dt=F32):
        return nc.dram_tensor(name, list(shape), dt, kind="ExternalInput").ap()

    xfull = {"s": din("xs", [8192, D]), "p": din("xp", [4096, D])}
    xown = {"s": din("xo", [2048, D]), "p": xfull["p"]}
    ln_w = din("ln_w", [D])
    w_in = din("w_in", [D, INW])
    q_norm = din("q_norm", [64])
    k_norm = din("k_norm", [64])
    w_f = din("w_f", [4, 128, 128])
    b_f = din("b_f", [4, 128])
    w_out = din("w_out", [D, D])
    fnorm = din("fnorm", [D])
    rk = {"s": din("rk_s", [64, 128, 64]), "p": din("rk_p", [32, 128, 64])}
    rq = {"s": din("rq_s", [16, 128, 64]), "p": din("rq_p", [32, 128, 64])}
    f1_d = din("f1", [128, 256], BF16)
    t1_d = {"s": din("t1_s", [128, 256]), "p": din("t1_p", [128, 256])}
    t2_d = {"s": din("t2_s", [128, 256]), "p": din("t2_p", [128, 256])}
    ga_d = {"s": din("ga_s", [128, 64], BF16), "p": din("ga_p", [128, 256], BF16)}
    gb_d = {"s": din("gb_s", [128, 64], BF16), "p": din("gb_p", [128, 256], BF16)}
    gn_d = {"s": din("gn_s", [128, 64], BF16), "p": din("gn_p", [128, 256], BF16)}
    cc_d = din("cc", [128, 128])
    sc_d = din("sc", [128, 128])
    yout = {"s": nc.dram_tensor("yo", [2048, D], F32, kind="ExternalOutput").ap(),
            "p": nc.dram_tensor("yp", [4096, D], F32, kind="ExternalOutput").ap()}

    ARENA_BYTES = 207 * 1024
    st = contextlib.ExitStack()
    arena = st.enter_context(nc.sbuf_tensor("arena", [128, ARENA_BYTES // 4], F32))
    psum = st.enter_context(nc.psum_tensor("psum", [128, 4096], F32))
    P = Prog(nc)

    class Alloc:
        def __init__(self, base=0):
            self.off = base

        def get(self, free_shape, dt):
            nel = int(np.prod(free_shape))
            sz = nel * (4 if dt == F32 else 2)
            sz = (sz + 63) // 64 * 64
            a = arena[:, self.off // 4:(self.off + sz) // 4]
            self.off += sz
            assert self.off <= ARENA_BYTES, ("SBUF arena overflow", self.off)
            if dt != F32:
                a = a.bitcast(dt)
            a = a[:, 0:nel]
            if len(free_shape) == 2:
                a = a.rearrange("p (a b) -> p a b", b=free_shape[1])
            elif len(free_shape) == 3:
                a = a.rearrange("p (a b c) -> p a b c", b=free_shape[1], c=free_shape[2])
            return a

    def bank(i, n=1):
        return psum[:, 512 * i:512 * (i + n)]

    PB = [Buf("pb%d" % i, excl=True) for i in range(8)]

    A0 = Alloc(0)
    ident = A0.get((128,), BF16)
    f1 = A0.get((256,), BF16)
    lnw_col = A0.get((8,), F32)
    gq = A0.get((64,), F32)
    gk = A0.get((64,), F32)
    bfc = A0.get((4,), F32)
    fn_bc = A0.get((1024,), F32)
    qc = A0.get((4, 2, 128), BF16)
    cc_t = A0.get((128,), F32)
    sc_t = A0.get((128,), F32)
    epsc = A0.get((1,), F32)
    wf_t = A0.get((4, 128), F32)
    neg1 = A0.get((1,), F32)
    B_ident, B_f1, B_small, B_fn, B_qc, B_cs = Buf(), Buf(), Buf(), Buf(), Buf(), Buf()
    PERSIST_END = A0.off

    P.op("pool", lambda e: e.memset(ident, 1.0), writes=[B_ident])
    P.op("pool", lambda e: e.affine_select(ident, ident, [[-1, 128]], ALU.is_equal, 0.0, base=0, channel_multiplier=1),
         reads=[B_ident], writes=[B_ident])
    P.op("pool", lambda e: e.memset(epsc, EPS), writes=[B_small])
    P.op("pool", lambda e: e.memset(neg1, -1.0), writes=[B_small])
    P.op("pool", lambda e: e.dma_start(out=f1, in_=f1_d), writes=[B_f1], dma=True)
    P.op("pool", lambda e: e.dma_start(out=lnw_col, in_=ln_w.rearrange("(k p) -> p k", p=128), allow_slow_non_contiguous=True),
         writes=[B_small], dma=True)
    P.op("pool", lambda e: e.dma_start(out=gq, in_=q_norm.partition_broadcast(128)), writes=[B_small], dma=True)
    P.op("pool", lambda e: e.dma_start(out=gk, in_=k_norm.partition_broadcast(128)), writes=[B_small], dma=True)
    P.op("pool", lambda e: e.dma_start(out=bfc, in_=b_f.rearrange("g d -> d g"), allow_slow_non_contiguous=True),
         writes=[B_small], dma=True)
    P.op("pool", lambda e: e.dma_start(out=fn_bc, in_=fnorm.partition_broadcast(128)), writes=[B_fn], dma=True)
    P.op("pool", lambda e: e.dma_start(out=cc_t, in_=cc_d), writes=[B_cs], dma=True)
    P.op("pool", lambda e: e.dma_start(out=sc_t, in_=sc_d), writes=[B_cs], dma=True)

    B_wf = Buf()
    P.op("pool", lambda e: e.dma_start(out=wf_t, in_=w_f.rearrange("g c d -> c g d")), writes=[B_wf], dma=True)
    for g in range(4):
        for ri, cst in enumerate((cc_t, sc_t)):
            bk = (g * 2 + ri) % 8
            P.op("pe", lambda e, cst=cst, g=g, bk=bk: e.matmul(bank(bk)[:, 0:128], cst, wf_t[:, g, :], start=True, stop=True),
                 reads=[B_cs, B_wf], writes=[PB[bk]])
            P.op("dve", lambda e, g=g, ri=ri, bk=bk: e.tensor_copy(qc[:, g, ri, :], bank(bk)[:, 0:128]),
                 reads=[PB[bk]], writes=[B_qc])
    stopped = False
    try:
        chk("setup", [("qc", qc.rearrange("p a b c -> p (a b c)"), [128, 1024], BF16), ("ident", ident, [128, 128], BF16),
                      ("lnw", lnw_col, [128, 8], F32), ("gq", gq, [128, 64], F32), ("bfc", bfc, [128, 4], F32), ("fn", fn_bc, [128, 1024], F32)])
    except _Stop:
        stopped = True

    def rms_rstd(x_ap, x_buf, junk, junk_buf, ss, lnv, rstd, sbuf, ncols, sq_eng="act"):
        if sq_eng == "act":
            P.op("act", lambda e: e.activation(junk, x_ap, AF.Square, accum_out=ss), reads=[x_buf], writes=[junk_buf, sbuf], n=ncols)
        else:
            P.op("dve", lambda e: e.tensor_tensor(junk, x_ap, x_ap, ALU.mult), reads=[x_buf], writes=[junk_buf], n=ncols)
            P.op("dve", lambda e: e.tensor_reduce(ss, junk, AX.X, ALU.add), reads=[junk_buf], writes=[sbuf], n=ncols)
        P.op("act", lambda e: e.activation(lnv, ss, AF.Ln, bias=epsc, scale=1.0 / ncols), reads=[sbuf, B_small], writes=[sbuf], n=1)
        P.op("act", lambda e: e.activation(rstd, lnv, AF.Exp, scale=-0.5), reads=[sbuf], writes=[sbuf], n=1)

    def normrope(src, src_bufs, T, H, gains, tab, tab_buf, outb, out_buf, wk, reng="pool", evac=False):
        N = T * H * 64
        xf, sq, ta, tb = wk["xf"][:, 0:N], wk["sq"][:, 0:N], wk["ta"][:, 0:N // 2], wk["tb"][:, 0:N // 2]
        ssq, lnq, rsq = wk["ssq"][:, 0:T * H], wk["lnq"][:, 0:T * H], wk["rsq"][:, 0:T * H]
        BW = wk["buf"]
        xf3 = xf.rearrange("p (t c) -> p t c", t=T)
        sq3 = sq.rearrange("p (t c) -> p t c", t=T)
        if evac:
            P.op("dve", lambda e, psrc=src: e.tensor_copy(xf3, psrc), reads=src_bufs, writes=[BW["xf"]], n=N)
            src, src_bufs = xf3, [BW["xf"]]
        if evac:
            P.op("dve", lambda e: e.tensor_tensor(sq3, src, src, ALU.mult), reads=src_bufs, writes=[BW["sq"]], n=N)
        else:
            P.op("act", lambda e: e.activation(sq3, src, AF.Square), reads=src_bufs, writes=[BW["sq"]], n=N)
        P.op("dve", lambda e: e.tensor_reduce(ssq, sq.rearrange("p (h d) -> p h d", d=64), AX.X, ALU.add),
             reads=[BW["sq"]], writes=[BW["st"]], n=N)
        P.op("act", lambda e: e.activation(lnq, ssq, AF.Ln, bias=epsc, scale=1.0 / 64), reads=[BW["st"], B_small], writes=[BW["st"]], n=8)
        P.op("act", lambda e: e.activation(rsq, lnq, AF.Exp, scale=-0.5), reads=[BW["st"]], writes=[BW["st"]], n=8)
        P.op("dve", lambda e: e.tensor_tensor(sq3.rearrange("p t (h d) -> p t h d", d=64), src.rearrange("p t (h d) -> p t h d", d=64),
                                              rsq.rearrange("p (t h) -> p t h", t=T).unsqueeze(3).to_broadcast([128, T, H, 64]), ALU.mult),
             reads=src_bufs + [BW["st"]], writes=[BW["sq"]], n=N)
        P.op("dve", lambda e: e.tensor_tensor(xf.rearrange("p (h d) -> p h d", d=64), sq.rearrange("p (h d) -> p h d", d=64),
                                              gains.unsqueeze(1).to_broadcast([128, T * H, 64]), ALU.mult),
             reads=[BW["sq"], B_small], writes=[BW["xf"]], n=N)
        x4 = xf.rearrange("p (t h i two) -> p t h i two", t=T, h=H, two=2)
        o4 = outb.rearrange("p (t h i two) -> p t h i two", t=T, h=H, two=2)
        x1, x2 = x4[:, :, :, :, 0], x4[:, :, :, :, 1]
        cosb = tab[:, :, 0:32].unsqueeze(2).to_broadcast([128, T, H, 32])
        sinb = tab[:, :, 32:64].unsqueeze(2).to_broadcast([128, T, H, 32])
        ta4 = ta.rearrange("p (t h i) -> p t h i", t=T, h=H)
        tb4 = tb.rearrange("p (t h i) -> p t h i", t=T, h=H)
        h2 = N // 2
        P.op(reng, lambda e: e.tensor_tensor(ta4, x1, cosb, ALU.mult), reads=[BW["xf"], tab_buf], writes=[BW["ta"]], n=h2)
        P.op(reng, lambda e: e.tensor_tensor(tb4, x2, sinb, ALU.mult), reads=[BW["xf"], tab_buf], writes=[BW["tb"]], n=h2)
        P.op(reng, lambda e: e.tensor_tensor(o4[:, :, :, :, 0], ta4, tb4, ALU.subtract), reads=[BW["ta"], BW["tb"]], writes=[out_buf], n=h2)
        P.op(reng, lambda e: e.tensor_tensor(ta4, x1, sinb, ALU.mult), reads=[BW["xf"], tab_buf], writes=[BW["ta"]], n=h2)
        P.op(reng, lambda e: e.tensor_tensor(tb4, x2, cosb, ALU.mult), reads=[BW["xf"], tab_buf], writes=[BW["tb"]], n=h2)
        P.op(reng, lambda e: e.tensor_tensor(o4[:, :, :, :, 1], ta4, tb4, ALU.add), reads=[BW["ta"], BW["tb"]], writes=[out_buf], n=h2)

    wq = [0]

    def load_convert_win(dst, dst_buf, kc, sections, stage, stage_buf):
        for (c0, n, d0, perm) in sections:
            wq[0] += 1
            P.op("sp" if wq[0] % 2 else "pool", lambda e, c0=c0, n=n: e.dma_start(out=stage[:, 0:n], in_=w_in[kc * 128:(kc + 1) * 128, c0:c0 + n]),
                 writes=[stage_buf], dma=True, nbytes=128 * n * 4)
            if perm:
                o = dst[:, kc, d0:d0 + n].rearrange("p (hp ab d) -> p hp ab d", hp=4, ab=2)
                i = stage[:, 0:n].rearrange("p (ab hp d) -> p hp ab d", hp=4, ab=2)
            else:
                o = dst[:, kc, d0:d0 + n]
                i = stage[:, 0:n]
            P.op("dve", lambda e, o=o, i=i: e.tensor_scalar(o, i, lnw_col[:, kc:kc + 1], None, ALU.mult),
                 reads=[stage_buf, B_small], writes=[dst_buf], n=n)

    def wk_set(A, N):
        return dict(xf=A.get((N,), F32), sq=A.get((N,), F32), ta=A.get((N // 2,), F32), tb=A.get((N // 2,), F32),
                    ssq=A.get((16,), F32), lnq=A.get((16,), F32), rsq=A.get((16,), F32),
                    buf=dict(xf=Buf(), sq=Buf(), ta=Buf(), tb=Buf(), st=Buf()))

    def run_job(job):
        nm, S, J, OWN = job["nm"], job["S"], job["J"], job["own"]
        r = 128 // J
        NT_OWN = OWN // 128
        K2 = NT_OWN
        N2 = 2 * K2 * r
        NCH = OWN // 512
        xf_d, xo_d, y_d = xfull[nm], xown[nm], yout[nm]
        x_str = xf_d.rearrange("(p j) c -> j p c", j=J)
        x_nat = xo_d.rearrange("(t p) c -> t p c", p=128)
        y_nat = y_d.rearrange("(t p) c -> t p c", p=128)
        P.strm = 0

        AJ = Alloc(PERSIST_END)
        kT = AJ.get((S,), BF16)
        vx = AJ.get((J, 192), BF16)
        foT = AJ.get((4, OWN), BF16)
        t1 = AJ.get((256,), F32)
        t2 = AJ.get((256,), F32)
        ga = AJ.get((N2,), BF16)
        gb = AJ.get((N2,), BF16)
        B_kT = [Buf() for _ in range(J)]
        B_vx = [Buf() for _ in range(J)]
        B_fo = [Buf() for _ in range(NCH)]
        B_tw, B_g = Buf(), Buf()
        JOB_END = AJ.off

        P.op("pool", lambda e: e.memset(vx, 1.0), writes=B_vx, n=J * 192)
        P.op("pool", lambda e: e.dma_start(out=t1, in_=t1_d[nm]), writes=[B_tw], dma=True, nbytes=131072)
        P.op("pool", lambda e: e.dma_start(out=t2, in_=t2_d[nm]), writes=[B_tw], dma=True, nbytes=131072)
        P.op("pool", lambda e: e.dma_start(out=ga, in_=ga_d[nm]), writes=[B_g], dma=True, nbytes=65536)
        P.op("pool", lambda e: e.dma_start(out=gb, in_=gb_d[nm]), writes=[B_g], dma=True, nbytes=65536)

        AF_ = Alloc(JOB_END)
        U = AF_.get((J, 512), BF16)
        xg = [AF_.get((K2, 2, 128), BF16) for _ in range(2)]
        F2_BASE = AF_.off
        w1 = AF_.get((8, 768), BF16)
        NXT = 3
        xt = [AF_.get((1024,), F32) for _ in range(NXT)]
        hb = [AF_.get((1024,), BF16) for _ in range(2)]
        hT = [AF_.get((8, 128), BF16) for _ in range(2)]
        junk = AF_.get((1024,), BF16)
        st_s = [AF_.get((4,), F32) for _ in range(2)]
        rt = [AF_.get((2, 64), F32) for _ in range(2)]
        wkk = [wk_set(AF_, 256) for _ in range(2)]
        kb = [AF_.get((256,), BF16) for _ in range(2)]
        _as = Alloc(JOB_END + J * 512 * 2)
        NSTG = 8 if nm == "s" else 8
        stage = [_as.get((512,), F32) for _ in range(NSTG)]
        AF2 = Alloc(F2_BASE)
        NS1 = 3
        p12 = [AF2.get((2, 2, 256), BF16) for _ in range(NS1)]
        y2 = [AF2.get((2, 2, 128), BF16) for _ in range(NS1)]
        xT3 = [AF2.get((2, 512), BF16) for _ in range(2)]
        gn = AF2.get((N2,), BF16)
        y1s = [AF2.get((2, 256), BF16) for _ in range(NS1)]
        t1b = AF2.get((256,), BF16)
        t2b = AF2.get((256,), BF16)
        B_y1s = [Buf() for _ in range(NS1)]
        B_p2 = [Buf() for _ in range(NS1)]
        B_p2b = [Buf() for _ in range(NS1)]
        B_twb = Buf()
        B_w1, B_U = Buf(), Buf()
        B_xg = [Buf(), Buf()]
        B_xt = [Buf() for _ in range(NXT)]
        B_hb, B_hT, B_st, B_rt, B_kb = ([Buf(), Buf()] for _ in range(5))
        B_junk = Buf()
        B_p12 = [Buf() for _ in range(NS1)]
        B_stage = [Buf() for _ in range(NSTG)]
        B_y2 = [Buf() for _ in range(NS1)]
        B_xT3 = [Buf(), Buf()]

        for kc in range(8):
            load_convert_win(w1, B_w1, kc, [(512, 256, 0, False)], stage[(2 * kc) % NSTG], B_stage[(2 * kc) % NSTG])
            load_convert_win(w1, B_w1, kc, [(1280, 512, 256, False)], stage[(2 * kc + 1) % NSTG], B_stage[(2 * kc + 1) % NSTG])

        chk(nm + "_w1", [("w1", w1.rearrange("p a b -> p (a b)"), [128, 8 * 768], BF16)])
        for jp in range(J // 2):
            bkv = 2 + (jp % 2)
            rts = jp % 2
            P.op("pool", lambda e, jp=jp, rts=rts: e.dma_start(out=rt[rts], in_=rk[nm][2 * jp:2 * jp + 2].rearrange("t p c -> p t c")),
                 writes=[B_rt[rts]], dma=True, nbytes=65536)
            for tl in range(2):
                j = 2 * jp + tl
                xs_, hs = j % NXT, j % 2
                P.op("sp", lambda e, j=j, xs_=xs_: e.dma_start(out=xt[xs_], in_=x_str[j]), writes=[B_xt[xs_]], dma=True, nbytes=524288)
                ss, lnv, rstd = st_s[hs][:, 0:1], st_s[hs][:, 1:2], st_s[hs][:, 2:3]
                rms_rstd(xt[xs_], B_xt[xs_], junk, B_junk, ss, lnv, rstd, B_st[hs], D)
                P.op("dve", lambda e, xs_=xs_, hs=hs, rstd=rstd: e.tensor_scalar(hb[hs], xt[xs_], rstd, None, ALU.mult),
                     reads=[B_xt[xs_], B_st[hs]], writes=[B_hb[hs]], n=600)
                bt = j % 2
                psT = bank(bt).bitcast(BF16)
                for kc in range(8):
                    P.op("pe", lambda e, kc=kc, hs=hs, psT=psT: e.transpose(psT[:, kc * 128:(kc + 1) * 128], hb[hs][:, kc * 128:(kc + 1) * 128], ident),
                         reads=[B_hb[hs], B_ident], writes=[PB[bt]], n=128)
                P.op("dve", lambda e, hs=hs, psT=psT: e.tensor_copy(hT[hs].rearrange("p a b -> p (a b)"), psT),
                     reads=[PB[bt]], writes=[B_hT[hs]], n=700)
                bu = 4 + (j % 3)
                for kc in range(8):
                    P.op("pe", lambda e, kc=kc, hs=hs, tl=tl, bkv=bkv: e.matmul(bank(bkv)[:, tl * 256:(tl + 1) * 256], hT[hs][:, kc, :], w1[:, kc, 0:256],
                                                                       start=(kc == 0), stop=(kc == 7)),
                         reads=[B_hT[hs], B_w1], writes=[PB[bkv]], n=256)
                    P.op("pe", lambda e, kc=kc, hs=hs, bu=bu: e.matmul(bank(bu), hT[hs][:, kc, :], w1[:, kc, 256:768], start=(kc == 0), stop=(kc == 7)),
                         reads=[B_hT[hs], B_w1], writes=[PB[bu]], n=512)
                P.op("act", lambda e, j=j, bu=bu: e.activation(U[:, j, :], bank(bu), AF.Copy), reads=[PB[bu]], writes=[B_U], n=512)
                P.op("act", lambda e, j=j, tl=tl, bkv=bkv: e.activation(
                    vx[:, j, :].rearrange("p (a b) -> p a b", b=64)[:, 0:3:2, :],
                    bank(bkv)[:, tl * 256 + 128:(tl + 1) * 256].rearrange("p (a b) -> p a b", b=64), AF.Copy),
                    reads=[PB[bkv]], writes=[B_vx[j]], n=128)
            ksrc = bank(bkv).rearrange("p (t c) -> p t c", t=2)[:, :, 0:128]
            kslot = jp % 2
            normrope(ksrc, [PB[bkv]], 2, 2, gk, rt[rts], B_rt[rts], kb[kslot], B_kb[kslot], wkk[kslot])
            psK = bank(7).bitcast(BF16)
            for tl in range(2):
                P.op("pe", lambda e, tl=tl, kslot=kslot, psK=psK: e.transpose(psK[:, tl * 128:(tl + 1) * 128], kb[kslot][:, tl * 128:(tl + 1) * 128], ident),
                     reads=[B_kb[kslot], B_ident], writes=[PB[7]], n=128)
            P.op("dve", lambda e, jp=jp, psK=psK: e.tensor_copy(kT[:, jp * 256:(jp + 1) * 256], psK[:, 0:256]),
                 reads=[PB[7]], writes=[B_kT[2 * jp], B_kT[2 * jp + 1]], n=256)

        chk(nm + "_pass1", [("kT", kT, [128, S], BF16), ("vx", vx.rearrange("p a b -> p (a b)"), [128, J * 192], BF16),
                            ("U", U.rearrange("p a b -> p (a b)"), [128, 512 * J], BF16)])
        P.barrier()
        P.op("pool", lambda e: e.dma_start(out=gn, in_=gn_d[nm]), writes=[B_g], dma=True, nbytes=65536)
        P.op("dve", lambda e: e.tensor_copy(t1b, t1), reads=[B_tw], writes=[B_twb], n=256)
        P.op("dve", lambda e: e.tensor_copy(t2b, t2), reads=[B_tw], writes=[B_twb], n=256)
        NB = 128 // r
        PER2 = 512 // N2
        for g in range(4):
            xs = g % 2
            for bp in range(NB // 2):
                s1 = (g * (NB // 2) + bp) % NS1
                b1 = 0 + s1
                for bl in range(2):
                    cb = bp * 2 + bl
                    for cl in range(r):
                        cidx = g * 128 + cb * r + cl
                        P.op("pe", lambda e, bl=bl, cl=cl, cidx=cidx, b1=b1: e.matmul(bank(b1)[cl * J:(cl + 1) * J, bl * 256:(bl + 1) * 256],
                                                                             U.rearrange("p j c -> p (j c)")[:, cidx:J * 512:512], f1, start=True, stop=True,
                                                                             tile_position=(0, cl * J)),
                             reads=[B_U, B_f1], writes=[PB[b1]], n=256 // r + 40)
                pv = bank(b1).rearrange("p (b c) -> p b c", b=2)
                P.op("act", lambda e, s1=s1, pv=pv: e.activation(y1s[s1], pv, AF.Copy), reads=[PB[b1]], writes=[B_y1s[s1]], n=512)
                P.op("dve", lambda e, s1=s1: e.tensor_tensor(p12[s1][:, 0, :, :], y1s[s1], t1b.unsqueeze(1).to_broadcast([128, 2, 256]), ALU.mult),
                     reads=[B_y1s[s1], B_twb], writes=[B_p12[s1]], n=300)
                P.op("pool", lambda e, s1=s1: e.tensor_tensor(p12[s1][:, 1, 1, :], y1s[s1][:, 1, :], t2b, ALU.mult),
                     reads=[B_y1s[s1], B_twb], writes=[B_p2[s1]], n=256)
                P.op("dve", lambda e, s1=s1: e.tensor_tensor(p12[s1][:, 1, 0, :], y1s[s1][:, 0, :], t2b, ALU.mult),
                     reads=[B_y1s[s1], B_twb], writes=[B_p2b[s1]], n=200)
                for bl in range(2):
                    cb = bp * 2 + bl
                    grp2 = cb // PER2
                    pos2 = cb % PER2
                    b2 = 3 + (grp2 % 2)
                    terms = ((0, 0, ga), (1, 128, gn), (1, 0, gb), (0, 128, gb))
                    for ti, (wh, c0, gm) in enumerate(terms):
                        P.op("pe", lambda e, bl=bl, s1=s1, b2=b2, pos2=pos2, wh=wh, c0=c0, gm=gm, ti=ti: e.matmul(
                            bank(b2)[:, pos2 * N2:(pos2 + 1) * N2], p12[s1][:, wh, bl, c0:c0 + 128], gm, start=(ti == 0), stop=(ti == 3)),
                             reads=[B_p12[s1], B_p2[s1], B_p2b[s1], B_g], writes=[PB[b2]], n=N2)
                    if pos2 == PER2 - 1:
                        cb0 = grp2 * PER2
                        for ri in range(2):
                            src = bank(b2).rearrange("p (b ri k c) -> p b ri k c", b=PER2, ri=2, k=K2)[:, :, ri, :, :]
                            dst = xg[xs][:, :, ri, cb0 * r:(cb0 + PER2) * r].rearrange("p k (b c) -> p b k c", b=PER2)
                            P.op("act", lambda e, src=src, dst=dst: e.activation(dst, src, AF.Copy), reads=[PB[b2]], writes=[B_xg[xs]], n=256)
            for ch in range(NCH):
                s3 = ch % 2
                b3 = 5
                psT3 = bank(b3).bitcast(BF16).rearrange("p (ri t k) -> p ri t k", ri=2, t=4)
                for t in range(4):
                    for ri in range(2):
                        P.op("pe", lambda e, t=t, ri=ri, ch=ch, xs=xs, psT3=psT3: e.transpose(psT3[:, ri, t, :], xg[xs][:, ch * 4 + t, ri, :], ident),
                             reads=[B_xg[xs], B_ident], writes=[PB[b3]], n=128)
                P.op("act", lambda e, s3=s3, b3=b3: e.activation(xT3[s3].rearrange("p a b -> p (a b)"), bank(b3).bitcast(BF16), AF.Copy),
                     reads=[PB[b3]], writes=[B_xT3[s3]], n=1024)
                b4 = 6 + s3
                for ri in range(2):
                    P.op("pe", lambda e, ri=ri, s3=s3, b4=b4, g=g: e.matmul(bank(b4), qc[:, g, ri, :], xT3[s3][:, ri, :], start=(ri == 0), stop=(ri == 1)),
                         reads=[B_qc, B_xT3[s3]], writes=[PB[b4]], n=512)
                P.op("act", lambda e, b4=b4, g=g, ch=ch: e.activation(foT[:, g, ch * 512:(ch + 1) * 512], bank(b4), AF.Identity, bias=bfc[:, g:g + 1]),
                     reads=[PB[b4], B_small], writes=[B_fo[ch]], n=512)
        chk(nm + "_four", [("foT", foT.rearrange("p a b -> p (a b)"), [128, 4 * OWN], BF16),
                           ("xg1", xg[1].rearrange("p a b c -> p (a b c)"), [128, K2 * 256], BF16)])
        P.barrier()

        A2 = Alloc(JOB_END)
        w2 = A2.get((8, 1536), BF16)
        wo = A2.get((8, 1024), BF16)
        xt2 = [A2.get((1024,), F32) for _ in range(2)]
        hb2 = [A2.get((1024,), BF16) for _ in range(2)]
        hTc = A2.get((8, 512), BF16)
        junk2 = A2.get((1024,), BF16)
        st2 = [A2.get((4,), F32) for _ in range(4)]
        rt2 = [A2.get((2, 64), F32) for _ in range(2)]
        TP = 2 if nm == "p" else 1
        wkq = [wk_set(A2, 512 * TP)]
        qb = [A2.get((512 * TP,), BF16) for _ in range(2)]
        qT = [A2.get((4, 512), BF16) for _ in range(2)]
        sga = [A2.get((4, 512), BF16) for _ in range(2)]
        sgf = [A2.get((4, 512), BF16) for _ in range(2)]
        pT = [A2.get((1024,), BF16) for _ in range(3)]
        _yoff = A2.off
        yT = [A2.get((8, 512), BF16) for _ in range(2)]
        oab = A2.get((2, 512), F32)
        NEG = 1 if nm == "p" else 2
        eg = [A2.get((512,), F32) for _ in range(NEG)]
        B_eg = [Buf() for _ in range(NEG)]
        rc = A2.get((512,), F32)
        rr = [A2.get((1024,), F32) for _ in range(2)]
        _ay = Alloc(_yoff)
        stage2 = rr + [_ay.get((1024,), F32) for _ in range(4)]
        B_w2, B_wo, B_hTc, B_junk2, B_oab, B_rc = (Buf() for _ in range(6))
        B_xt2, B_hb2, B_rt2, B_qb, B_rr, B_qT, B_sga, B_sgf = ([Buf(), Buf()] for _ in range(8))
        B_st2 = [Buf() for _ in range(4)]
        B_stage2 = B_rr + [Buf() for _ in range(4)]
        B_pT = [Buf() for _ in range(3)]
        B_yT = [[Buf() for _ in range(8)] for _ in range(2)]

        nsl = [0]

        def nslot():
            nsl[0] += 1
            return nsl[0] % 6
        for c8 in range(8):
            sl = nslot()
            wq[0] += 1
            qn = "sp" if wq[0] % 2 else "pool"
            if c8 < 4:
                P.op(qn, lambda e, c8=c8, sl=sl: e.dma_start(out=stage2[sl][0:64, :], in_=w_out[64 * c8:64 * c8 + 64, :]),
                     writes=[B_stage2[sl]], dma=True, nbytes=262144)
                P.op(qn, lambda e, c8=c8, sl=sl: e.dma_start(out=stage2[sl][64:128, :], in_=w_out[256 + 64 * c8:256 + 64 * c8 + 64, :]),
                     writes=[B_stage2[sl]], dma=True, nbytes=262144)
            else:
                P.op(qn, lambda e, c8=c8, sl=sl: e.dma_start(out=stage2[sl], in_=w_out[512 + 128 * (c8 - 4):512 + 128 * (c8 - 3), :]),
                     writes=[B_stage2[sl]], dma=True, nbytes=524288)
            P.op("dve", lambda e, c8=c8, sl=sl: e.tensor_copy(wo[:, c8, :], stage2[sl]), reads=[B_stage2[sl]], writes=[B_wo], n=1024)
        for kc in range(8):
            for sec in [(0, 512, 0, True), (768, 512, 512, True), (1792, 512, 1024, False)]:
                sl = nslot()
                load_convert_win(w2, B_w2, kc, [sec], stage2[sl], B_stage2[sl])

        chk(nm + "_w2", [("w2", w2.rearrange("p a b -> p (a b)"), [128, 8 * 1536], BF16), ("wo", wo.rearrange("p a b -> p (a b)"), [128, 8 * 1024], BF16)])
        NKT = J
        cnt = dict(x=0, st=0, bg=0, step=0)

        def part_a(ch):
            P.strm = 1
            cs = ch % 2
            for tp in range(4 // TP):
                qs = (ch * (4 // TP) + tp) % 2
                tg0 = ch * 4 + tp * TP
                P.op("pool", lambda e, tg0=tg0, qs=qs: e.dma_start(out=rt2[qs][:, 0:TP, :], in_=rq[nm][tg0:tg0 + TP].rearrange("t p c -> p t c")),
                     writes=[B_rt2[qs]], dma=True, nbytes=32768 * TP)
                banks = []
                for tl in range(TP):
                    t = tp * TP + tl
                    tg = ch * 4 + t
                    xs_, hs = cnt["x"] % 2, tg % 2
                    cnt["x"] += 1
                    sts = cnt["st"] % 4
                    cnt["st"] += 1
                    P.op("sp", lambda e, tg=tg, xs_=xs_: e.dma_start(out=xt2[xs_], in_=x_nat[tg]), writes=[B_xt2[xs_]], dma=True, nbytes=524288)
                    ss, lnv, rstd = st2[sts][:, 0:1], st2[sts][:, 1:2], st2[sts][:, 2:3]
                    rms_rstd(xt2[xs_], B_xt2[xs_], junk2, B_junk2, ss, lnv, rstd, B_st2[sts], D, sq_eng="dve")
                    P.op("dve", lambda e, xs_=xs_, hs=hs, rstd=rstd: e.tensor_scalar(hb2[hs], xt2[xs_], rstd, None, ALU.mult),
                         reads=[B_xt2[xs_], B_st2[sts]], writes=[B_hb2[hs]], n=600)
                    if TP == 1:
                        ba, bb = 6 + (t % 2), 7 - (t % 2)
                    else:
                        ba = bb = 6 + tl
                    banks.append(bb)
                    psT = bank(ba).bitcast(BF16)
                    for kc in range(8):
                        P.op("pe", lambda e, kc=kc, hs=hs, psT=psT: e.transpose(psT[:, kc * 128:(kc + 1) * 128], hb2[hs][:, kc * 128:(kc + 1) * 128], ident),
                             reads=[B_hb2[hs], B_ident], writes=[PB[ba]], n=128)
                    P.op("dve", lambda e, t=t, psT=psT: e.tensor_copy(hTc[:, :, t * 128:(t + 1) * 128], psT.rearrange("p (a b) -> p a b", a=8)),
                         reads=[PB[ba]], writes=[B_hTc], n=700)
                    for kc in range(8):
                        P.op("pe", lambda e, kc=kc, t=t, bb=bb: e.matmul(bank(bb), hTc[:, kc, t * 128:(t + 1) * 128], w2[:, kc, 0:512], start=(kc == 0), stop=(kc == 7)),
                             reads=[B_hTc, B_w2], writes=[PB[bb]], n=512)
                if TP == 1:
                    qsrc, qsb = bank(banks[0]).rearrange("p (t c) -> p t c", t=1), [PB[banks[0]]]
                else:
                    qsrc, qsb = bank(6, 2).rearrange("p (t c) -> p t c", t=2), [PB[6], PB[7]]
                normrope(qsrc, qsb, TP, 8, gq, rt2[qs][:, 0:TP, :], B_rt2[qs], qb[qs], B_qb[qs], wkq[0], evac=True)
                bq = 6 + (tp % 2) if TP == 1 else 6
                psQ = bank(bq).bitcast(BF16)
                for tl in range(TP):
                    for hp in range(4):
                        P.op("pe", lambda e, hp=hp, tl=tl, qs=qs, psQ=psQ: e.transpose(psQ[:, (tl * 4 + hp) * 128:(tl * 4 + hp + 1) * 128],
                                                                                    qb[qs][:, (tl * 4 + hp) * 128:(tl * 4 + hp + 1) * 128], ident),
                             reads=[B_qb[qs], B_ident], writes=[PB[bq]], n=128)
                t0 = tp * TP
                P.op("dve", lambda e, t0=t0, psQ=psQ, cs=cs: e.tensor_copy(
                    qT[cs][:, :, t0 * 128:(t0 + TP) * 128].rearrange("p h (tl k) -> p tl h k", tl=TP),
                    psQ[:, 0:512 * TP].rearrange("p (tl h k) -> p tl h k", tl=TP, h=4)),
                    reads=[PB[bq]], writes=[B_qT[cs]], n=400 * TP)
            if ch == 0:
                chk(nm + "_t0", [("qT", qT[0].rearrange("p a b -> p (a b)"), [128, 2048], BF16), ("hTc", hTc.rearrange("p a b -> p (a b)"), [128, 4096], BF16)])
            for fc in range(8):
                bg = 6 + (cnt["bg"] % 2)
                cnt["bg"] += 1
                for kc in range(8):
                    P.op("pe", lambda e, kc=kc, fc=fc, bg=bg: e.matmul(bank(bg), w2[:, kc, 512 + fc * 128:512 + (fc + 1) * 128], hTc[:, kc, :],
                                                                   start=(kc == 0), stop=(kc == 7)),
                         reads=[B_hTc, B_w2], writes=[PB[bg]], n=512)
                es = cnt["bg"] % NEG
                dst = sga[cs][:, fc, :] if fc < 4 else sgf[cs][:, fc - 4, :]
                dbuf = B_sga[cs] if fc < 4 else B_sgf[cs]
                P.op("act", lambda e, bg=bg, es=es: e.activation(eg[es], bank(bg), AF.Exp, scale=-1.0), reads=[PB[bg]], writes=[B_eg[es]], n=512)
                P.op("dve", lambda e, es=es: e.tensor_scalar(eg[es], eg[es], 1.0, None, ALU.add), reads=[B_eg[es]], writes=[B_eg[es]], n=300)
                P.op("dve", lambda e, es=es: e.reciprocal(eg[es], eg[es]), reads=[B_eg[es]], writes=[B_eg[es]], n=3000)
                P.op("dve", lambda e, bg=bg, es=es, dst=dst: e.tensor_tensor(dst, bank(bg), eg[es], ALU.mult), reads=[PB[bg], B_eg[es]], writes=[dbuf], n=512)
                if fc >= 4:
                    P.op("pool", lambda e, fc=fc, ch=ch, cs=cs: e.tensor_tensor(yT[cs][:, fc, :], foT[:, fc - 4, ch * 512:(ch + 1) * 512], sgf[cs][:, fc - 4, :], ALU.mult),
                         reads=[B_fo[ch], B_sgf[cs], B_w2, B_wo], writes=[B_yT[cs][fc]], n=512)
            if ch == 0:
                chk(nm + "_A0", [("qT", qT[0].rearrange("p a b -> p (a b)"), [128, 2048], BF16), ("sga", sga[0].rearrange("p a b -> p (a b)"), [128, 2048], BF16),
                                 ("sgf", sgf[0].rearrange("p a b -> p (a b)"), [128, 2048], BF16), ("yT", yT[0].rearrange("p a b -> p (a b)"), [128, 4096], BF16)])

        def part_b(ch):
            P.strm = 0
            cs = ch % 2
            steps = [(hp, kt) for hp in range(4) for kt in range(NKT)]

            def qk(i):
                hp, kt = steps[i]
                sb = 2 * (i % 2)
                P.op("pe", lambda e, kt=kt, sb=sb, hp=hp: e.matmul(bank(sb), kT[0:64, kt * 128:(kt + 1) * 128], qT[cs][0:64, hp, :], start=True, stop=True),
                     reads=[B_kT[kt], B_qT[cs]], writes=[PB[sb]], n=300)
                P.op("pe", lambda e, kt=kt, sb=sb, hp=hp: e.matmul(bank(sb + 1), kT[64:128, kt * 128:(kt + 1) * 128], qT[cs][64:128, hp, :], start=True, stop=True),
                     reads=[B_kT[kt], B_qT[cs]], writes=[PB[sb + 1]], n=300)

            def ex(i):
                sb = 2 * (i % 2)
                ps = i % 3
                P.op("act", lambda e, sb=sb, ps=ps: e.activation(pT[ps], bank(sb, 2), AF.Exp, scale=0.125),
                     reads=[PB[sb], PB[sb + 1]], writes=[B_pT[ps]], n=1024)

            def pv(i):
                hp, kt = steps[i]
                ps = i % 3
                P.op("pe", lambda e, kt=kt, ps=ps: e.matmul(bank(4), vx[:, kt, 0:128], pT[ps][:, 0:512], start=(kt == 0), stop=(kt == NKT - 1)),
                     reads=[B_vx[kt], B_pT[ps]], writes=[PB[4]], n=512)
                P.op("pe", lambda e, kt=kt, ps=ps: e.matmul(bank(5), vx[:, kt, 64:192], pT[ps][:, 512:1024], start=(kt == 0), stop=(kt == NKT - 1)),
                     reads=[B_vx[kt], B_pT[ps]], writes=[PB[5]], n=512)

            def finalize(hp):
                P.op("dve", lambda e: e.tensor_copy(oab.rearrange("p a b -> p (a b)"), bank(4, 2)), reads=[PB[4], PB[5]], writes=[B_oab], n=1024)
                P.op("dve", lambda e: e.tensor_copy(rc[0:64, :], oab[64:128, 0, :]), reads=[B_oab], writes=[B_rc], n=256)
                P.op("dve", lambda e: e.tensor_copy(rc[64:128, :], oab[0:64, 1, :]), reads=[B_oab], writes=[B_rc], n=256)
                P.op("dve", lambda e: e.reciprocal(rc, rc), reads=[B_rc], writes=[B_rc], n=3000)
                P.op("pool", lambda e, hp=hp: e.tensor_tensor(rc, rc, sga[cs][:, hp, :], ALU.mult), reads=[B_rc, B_sga[cs]], writes=[B_rc], n=512)
                P.op("dve", lambda e, hp=hp: e.tensor_tensor(yT[cs][0:64, hp, :], oab[0:64, 0, :], rc[0:64, :], ALU.mult),
                     reads=[B_oab, B_rc, B_w2, B_wo], writes=[B_yT[cs][hp]], n=512)
                P.op("dve", lambda e, hp=hp: e.tensor_tensor(yT[cs][64:128, hp, :], oab[64:128, 1, :], rc[64:128, :], ALU.mult),
                     reads=[B_oab, B_rc, B_w2, B_wo], writes=[B_yT[cs][hp]], n=512)

            NS = len(steps)
            qk(0)
            qk(1)
            for i in range(NS):
                ex(i)
                if i + 2 < NS:
                    qk(i + 2)
                pv(i)
                if steps[i][1] == NKT - 1:
                    finalize(steps[i][0])
            if ch == 0:
                chk(nm + "_B0", [("yT", yT[0].rearrange("p a b -> p (a b)"), [128, 4096], BF16)])

        def part_c(ch):
            P.strm = 1
            cs = ch % 2
            for t in range(4):
                tg = ch * 4 + t
                xs_, os_ = cnt["x"] % 2, tg % 2
                cnt["x"] += 1
                sts = cnt["st"] % 4
                cnt["st"] += 1
                P.op("sp", lambda e, tg=tg, xs_=xs_: e.dma_start(out=xt2[xs_], in_=x_nat[tg]), writes=[B_xt2[xs_]], dma=True, nbytes=524288)
                for half in range(2):
                    for c8 in range(8):
                        P.op("pe", lambda e, c8=c8, half=half, t=t: e.matmul(bank(6 + half), yT[cs][:, c8, t * 128:(t + 1) * 128], wo[:, c8, half * 512:(half + 1) * 512],
                                                                     start=(c8 == 0), stop=(c8 == 7)),
                             reads=[B_yT[cs][c8], B_wo], writes=[PB[6 + half]], n=512)
                P.op("dve", lambda e, xs_=xs_, os_=os_: e.tensor_tensor(rr[os_], bank(6, 2), xt2[xs_], ALU.add),
                     reads=[PB[6], PB[7], B_xt2[xs_], B_w2, B_wo], writes=[B_rr[os_]], n=1024)
                ss, lnv, rstd = st2[sts][:, 0:1], st2[sts][:, 1:2], st2[sts][:, 2:3]
                rms_rstd(rr[os_], B_rr[os_], junk2, B_junk2, ss, lnv, rstd, B_st2[sts], D, sq_eng="dve")
                P.op("dve", lambda e, os_=os_, rstd=rstd: e.scalar_tensor_tensor(rr[os_], rr[os_], rstd, fn_bc, ALU.mult, ALU.mult),
                     reads=[B_rr[os_], B_st2[sts], B_fn], writes=[B_rr[os_]], n=1024)
                P.op("sp", lambda e, tg=tg, os_=os_: e.dma_start(out=y_nat[tg], in_=rr[os_]), reads=[B_rr[os_]], dma=True, nbytes=524288)

        P.group_begin(); part_a(0); P.group_end(-1.0, -0.01)
        for ch in range(NCH):
            if ch + 1 < NCH:
                P.group_begin(); part_a(ch + 1); P.group_end(ch + 0.0, ch + 0.60)
            P.group_begin(); part_b(ch); P.group_end(float(ch), ch + 1.0)
            if ch + 1 < NCH:
                P.group_begin(); part_c(ch); P.group_end(ch + 1.0, ch + 1.5)
            else:
                P.group_begin(); part_c(ch); P.group_end(float(NCH), NCH + 1.0)
        P.strm = 0
        P.barrier()

    try:
        for job in JOBS:
            if not stopped:
                run_job(job)
    except _Stop:
        pass

    P.schedule()
    P.emit()
    st.close()
    return nc


_NC = None
_CONST = None


def _consts():
    global _CONST
    if _CONST is not None:
        return _CONST
    c = {}
    p = np.arange(128)
    c["rk_s"] = np.stack([rope_tab(64 * p + j) for j in range(64)]).astype(np.float32)
    c["rk_p"] = np.stack([rope_tab(32 * p + j) for j in range(32)]).astype(np.float32)
    c["rq_p"] = np.stack([rope_tab(128 * k2 + p) for k2 in range(32)]).astype(np.float32)
    c["rq_s"] = [np.stack([rope_tab(2048 * qr + 128 * k2 + p) for k2 in range(16)]).astype(np.float32) for qr in range(4)]
    c["f1"] = f1_mat().astype(np.float32).astype(bf)
    c["t1_s"], c["t2_s"] = twiddle(64)
    c["t1_p"], c["t2_p"] = twiddle(32)
    gs = [g_mats(64, list(range(16 * qr, 16 * qr + 16))) for qr in range(4)]
    c["ga_s"] = [g[0].astype(np.float32).astype(bf) for g in gs]
    c["gb_s"] = [g[1].astype(np.float32).astype(bf) for g in gs]
    c["gn_s"] = [(-g[0]).astype(np.float32).astype(bf) for g in gs]
    gp = g_mats(32, list(range(32)))
    c["ga_p"] = gp[0].astype(np.float32).astype(bf)
    c["gb_p"] = gp[1].astype(np.float32).astype(bf)
    c["gn_p"] = (-gp[0]).astype(np.float32).astype(bf)
    c["cc"], c["sc"] = cs_mats()
    _CONST = c
    return c


def kernel(x_prompt, x_sample, ln_w, w_in, q_norm, k_norm, w_fourier, b_fourier, w_out, final_norm):
    global _NC
    if _NC is None:
        _NC = build()
    nc = _NC
    c = _consts()
    f32 = np.float32
    x_prompt = np.asarray(x_prompt, f32)
    x_sample = np.asarray(x_sample, f32)
    shared = {
        "ln_w": np.ascontiguousarray(np.asarray(ln_w, f32)[0]),
        "w_in": np.ascontiguousarray(np.asarray(w_in, f32)[0]),
        "q_norm": np.ascontiguousarray(np.asarray(q_norm, f32)[0]),
        "k_norm": np.ascontiguousarray(np.asarray(k_norm, f32)[0]),
        "w_f": np.ascontiguousarray(np.asarray(w_fourier, f32)[0]),
        "b_f": np.ascontiguousarray(np.asarray(b_fourier, f32)[0]),
        "w_out": np.ascontiguousarray(np.asarray(w_out, f32)[0]),
        "fnorm": np.ascontiguousarray(np.asarray(final_norm, f32)),
        "rk_s": c["rk_s"], "rk_p": c["rk_p"], "rq_p": c["rq_p"], "f1": c["f1"],
        "t1_s": c["t1_s"], "t2_s": c["t2_s"], "t1_p": c["t1_p"], "t2_p": c["t2_p"],
        "ga_p": c["ga_p"], "gb_p": c["gb_p"], "gn_p": c["gn_p"], "cc": c["cc"], "sc": c["sc"],
    }
    in_maps = []
    for core in range(N_CORES):
        b, qr = core // 4, core % 4
        m = dict(shared)
        m["xp"] = np.ascontiguousarray(x_prompt[core])
        m["xs"] = np.ascontiguousarray(x_sample[b])
        m["xo"] = np.ascontiguousarray(x_sample[b, 2048 * qr:2048 * (qr + 1)])
        m["rq_s"] = c["rq_s"][qr]
        m["ga_s"] = c["ga_s"][qr]
        m["gb_s"] = c["gb_s"][qr]
        m["gn_s"] = c["gn_s"][qr]
        in_maps.append(m)
    res = run_bass_kernel_spmd(nc, in_maps, core_ids=list(range(N_CORES)))
    y_prompt = np.stack([np.asarray(res.results[core]["yp"], f32) for core in range(N_CORES)])
    y_sample = np.empty((2, 8192, D), f32)
    for core in range(N_CORES):
        b, qr = core // 4, core % 4
        y_sample[b, 2048 * qr:2048 * (qr + 1)] = np.asarray(res.results[core]["yo"], f32)
    return (y_prompt, y_sample)
```
